# Optimizing a Trainium2 kernel written in Bass

```python
import jax, jax.numpy as jnp
from jax import lax
import numpy as np

D_MODEL = 1024
BATCH = 2
SEQ = 8192
DEPTH = 2
DEC_BATCH = 8
DEC_SEQ = 64
PAST_LEN = 2048

CHUNK = 64
N_A_LAYERS = DEPTH // 2
N_B_LAYERS = DEPTH - N_A_LAYERS
POOL_WINDOWS = (2, 4, 8, 16)
N_POOL_GROUPS = len(POOL_WINDOWS)
POOL_GROUP = D_MODEL // N_POOL_GROUPS
POOL_HIST = max(POOL_WINDOWS) - 1
N_HEADS = 16
HEAD_DIM = D_MODEL // N_HEADS
HD = N_HEADS * HEAD_DIM
D_FF = ((8 * D_MODEL // 3 + 127) // 128) * 128
Q_BLOCK = 128
RMS_EPS = 1e-6
FORGET_BIAS_INIT = 2.0

kernel_name = "yoco_pool_fox_streaming_step"


def _rms_norm(x, g):
    xf = x.astype(jnp.float32)
    y = xf * lax.rsqrt(jnp.mean(xf * xf, axis=-1, keepdims=True) + RMS_EPS)
    return (y * g.astype(jnp.float32)).astype(x.dtype)


def _swiglu(h, w_in, w_out):
    g, u = jnp.split(h @ w_in, 2, axis=-1)
    return (jax.nn.silu(g) * u) @ w_out


def _pool_mix(hist, u, pos0, w_pool, scale):
    B, T, D = u.shape
    full = jnp.concatenate([hist.astype(jnp.float32), u.astype(jnp.float32)], axis=1)
    cs = jnp.concatenate([jnp.zeros((B, 1, D), jnp.float32), jnp.cumsum(full, axis=1)], axis=1)
    P = POOL_HIST
    end = cs[:, P + 1:P + 1 + T]
    pos = pos0 + jnp.arange(T)
    outs = []
    for g, w in enumerate(POOL_WINDOWS):
        sl = slice(g * POOL_GROUP, (g + 1) * POOL_GROUP)
        start = cs[:, P + 1 - w:P + 1 - w + T, sl]
        cnt = jnp.minimum(pos + 1, w).astype(jnp.float32)[None, :, None]
        mean = (end[..., sl] - start) / cnt
        outs.append(mean - u[..., sl].astype(jnp.float32))
    d = jnp.stack(outs, axis=2).astype(u.dtype)
    y = jnp.einsum('btgc,gcd->btgd', d, w_pool).reshape(B, T, D)
    return y * scale


def _forget_attention(q, k, v, c_q, c_k):
    B, Tq = q.shape[0], q.shape[1]
    Tk = k.shape[1]
    scale = HEAD_DIM ** -0.5
    k_pos = jnp.arange(Tk)
    q_pos = jnp.arange(Tq) + (Tk - Tq)
    c_kT = jnp.transpose(c_k, (0, 2, 1))

    def block(args):
        qb, cqb, posb = args
        s = jnp.einsum('bqhd,bkhd->bhqk', qb, k, preferred_element_type=jnp.float32) * scale
        decay = jnp.transpose(cqb, (0, 2, 1))[..., None] - c_kT[:, :, None, :]
        mask = k_pos[None, :] <= posb[:, None]
        s = jnp.where(mask, s + decay, -jnp.inf)
        p = jax.nn.softmax(s, axis=-1)
        return jnp.einsum('bhqk,bkhd->bqhd', p.astype(v.dtype), v)

    if Tq <= Q_BLOCK:
        return block((q, c_q, q_pos))
    nb = Tq // Q_BLOCK
    qb = q.reshape(B, nb, Q_BLOCK, N_HEADS, HEAD_DIM).transpose(1, 0, 2, 3, 4)
    cqb = c_q.reshape(B, nb, Q_BLOCK, N_HEADS).transpose(1, 0, 2, 3)
    pb = q_pos.reshape(nb, Q_BLOCK)
    o = lax.map(block, (qb, cqb, pb))
    return o.transpose(1, 0, 2, 3, 4).reshape(B, Tq, N_HEADS, HEAD_DIM)


def _trunk(x, pool_hist, past, weights):
    (ln_ffn1, ln_mix, ln_ffn2, w_ffn_in, w_ffn_out, w_pool, pool_scale,
     ln_kv, w_kv, w_fgate, b_fgate, w_q, w_o, ln_final) = weights
    B, T, _ = x.shape
    pos0 = 0 if past is None else past[0].shape[1]
    new_pool = []
    k_all = v_all = c_k = c_q = None
    k_new = v_new = logf_new = None
    for l in range(DEPTH):
        x = x + 0.5 * _swiglu(_rms_norm(x, ln_ffn1[l]), w_ffn_in[l, 0], w_ffn_out[l, 0])
        u = _rms_norm(x, ln_mix[l])
        if l < N_A_LAYERS:
            hist = pool_hist[l].astype(u.dtype)
            x = x + _pool_mix(hist, u, pos0, w_pool[l], pool_scale[l])
            new_pool.append(jnp.concatenate([hist, u], axis=1)[:, -POOL_HIST:])
        else:
            j = l - N_A_LAYERS
            q = (u @ w_q[j]).reshape(B, T, N_HEADS, HEAD_DIM)
            o = _forget_attention(q, k_all, v_all, c_q, c_k)
            x = x + o.reshape(B, T, HD) @ w_o[j]
        x = x + 0.5 * _swiglu(_rms_norm(x, ln_ffn2[l]), w_ffn_in[l, 1], w_ffn_out[l, 1])
        if l == N_A_LAYERS - 1:
            kv_in = _rms_norm(x, ln_kv)
            kv = kv_in @ w_kv
            k_new = kv[..., :HD].reshape(B, T, N_HEADS, HEAD_DIM)
            v_new = kv[..., HD:].reshape(B, T, N_HEADS, HEAD_DIM)
            logf_new = jax.nn.log_sigmoid((kv_in @ w_fgate).astype(jnp.float32) + b_fgate.astype(jnp.float32))
            if past is None:
                k_all, v_all, logf_all = k_new, v_new, logf_new
            else:
                k_all = jnp.concatenate([past[0].astype(k_new.dtype), k_new], axis=1)
                v_all = jnp.concatenate([past[1].astype(v_new.dtype), v_new], axis=1)
                logf_all = jnp.concatenate([past[2].astype(jnp.float32), logf_new], axis=1)
            c_k = jnp.cumsum(logf_all, axis=1)
            c_q = c_k[:, -T:]
    y = _rms_norm(x, ln_final)
    return y, jnp.stack(new_pool), k_new, v_new, logf_new.astype(x.dtype)


def setup_inputs(seed: int = 0) -> dict:
    key = jax.random.key(seed)
    ks = jax.random.split(key, 24)

    def nrm(k, shape, scale):
        return jax.random.normal(k, shape, jnp.float32) * scale

    def gain(k, shape):
        return 1.0 + nrm(k, shape, 0.05)

    return {
        'x_prompt': nrm(ks[0], (BATCH, SEQ, D_MODEL), 1.0),
        'x_sample': nrm(ks[1], (DEC_BATCH, DEC_SEQ, D_MODEL), 1.0),
        'cache_pool': nrm(ks[2], (N_A_LAYERS, DEC_BATCH, POOL_HIST, D_MODEL), 1.0),
        'cache_k': nrm(ks[3], (DEC_BATCH, PAST_LEN, N_HEADS, HEAD_DIM), 1.0),
        'cache_v': nrm(ks[4], (DEC_BATCH, PAST_LEN, N_HEADS, HEAD_DIM), 1.0),
        'cache_logf': jax.nn.log_sigmoid(FORGET_BIAS_INIT + nrm(ks[5], (DEC_BATCH, PAST_LEN, N_HEADS), 1.0)),
        'ln_ffn1': gain(ks[6], (DEPTH, D_MODEL)),
        'ln_mix': gain(ks[7], (DEPTH, D_MODEL)),
        'ln_ffn2': gain(ks[8], (DEPTH, D_MODEL)),
        'w_ffn_in': nrm(ks[9], (DEPTH, 2, D_MODEL, 2 * D_FF), D_MODEL ** -0.5),
        'w_ffn_out': nrm(ks[10], (DEPTH, 2, D_FF, D_MODEL), D_FF ** -0.5),
        'w_pool': nrm(ks[11], (N_A_LAYERS, N_POOL_GROUPS, POOL_GROUP, POOL_GROUP), POOL_GROUP ** -0.5),
        'pool_scale': gain(ks[12], (N_A_LAYERS, D_MODEL)),
        'ln_kv': gain(ks[13], (D_MODEL,)),
        'w_kv': nrm(ks[14], (D_MODEL, 2 * HD), D_MODEL ** -0.5),
        'w_fgate': nrm(ks[15], (D_MODEL, N_HEADS), D_MODEL ** -0.5),
        'b_fgate': FORGET_BIAS_INIT + nrm(ks[16], (N_HEADS,), 0.1),
        'w_q': nrm(ks[17], (N_B_LAYERS, D_MODEL, HD), D_MODEL ** -0.5),
        'w_o': nrm(ks[18], (N_B_LAYERS, HD, D_MODEL), HD ** -0.5),
        'ln_final': gain(ks[19], (D_MODEL,)),
    }


def reference(x_prompt, x_sample, cache_pool, cache_k, cache_v, cache_logf,
              ln_ffn1, ln_mix, ln_ffn2, w_ffn_in, w_ffn_out, w_pool, pool_scale,
              ln_kv, w_kv, w_fgate, b_fgate, w_q, w_o, ln_final):
    weights = (ln_ffn1, ln_mix, ln_ffn2, w_ffn_in, w_ffn_out, w_pool, pool_scale,
               ln_kv, w_kv, w_fgate, b_fgate, w_q, w_o, ln_final)
    prompt_hist = jnp.zeros((N_A_LAYERS, x_prompt.shape[0], POOL_HIST, D_MODEL), x_prompt.dtype)
    y_prompt, pool_prompt, k_prompt, v_prompt, logf_prompt = _trunk(x_prompt, prompt_hist, None, weights)
    y_sample, pool_sample, k_sample, v_sample, logf_sample = _trunk(
        x_sample, cache_pool, (cache_k, cache_v, cache_logf), weights)
    return (y_prompt, y_sample, pool_prompt, pool_sample, k_prompt, v_prompt, logf_prompt,
            k_sample, v_sample, logf_sample)
```

```python
import concourse.bass as bass
import concourse.mybir as mybir

ENGS = ("pe", "act", "dve", "pool", "sp")


class Sched:
    def __init__(self, nc, ring=8):
        self.nc = nc
        self.ops = {e: [] for e in ENGS}
        self.state_w = {}
        self.state_r = {}
        self.ring = ring
        self.dma_count = {"sp": 0, "pool": 0, "act": 0}
        self.dma_ev = {"sp": [], "pool": [], "act": []}
        self.nseq = {e: 0 for e in ENGS}
        self.needed = {e: set() for e in ENGS}

    def _collect(self, eng, reads, writes):
        deps = {}

        def add(d, own_ok):
            for k, v in d.items():
                if k == ("eng", eng) and not own_ok and eng == "pe":
                    continue
                if deps.get(k, 0) < v:
                    deps[k] = v
        for key in reads:
            add(self.state_w.get(key, {}), True)
        for key in writes:
            add(self.state_w.get(key, {}), False)
            add(self.state_r.get(key, {}), False)
        return deps

    def _commit(self, evkey, val, reads, writes):
        for key in reads:
            d = self.state_r.setdefault(key, {})
            if d.get(evkey, 0) < val:
                d[evkey] = val
        for key in writes:
            d = self.state_w.setdefault(key, {})
            if d.get(evkey, 0) < val:
                d[evkey] = val
            self.state_r[key] = {}

    def op(self, eng, fn, reads=(), writes=()):
        deps = self._collect(eng, reads, writes)
        seq = self.nseq[eng]
        self.nseq[eng] += 1
        for k, v in deps.items():
            if k[0] == "eng":
                self.needed[k[1]].add(v - 1)
        self.ops[eng].append(dict(fn=fn, deps=deps, seq=seq, dma=None))
        self._commit(("eng", eng), seq + 1, reads, writes)

    def dma(self, q, fn, reads=(), writes=()):
        deps = self._collect(q, reads, writes)
        i = self.dma_count[q]
        self.dma_count[q] += 1
        slot = i % self.ring
        val = 16 * (i // self.ring + 1)
        evkey = ("dma", q, slot)
        if i >= self.ring:
            if deps.get(evkey, 0) < val - 16:
                deps[evkey] = val - 16
        seq = self.nseq[q]
        self.nseq[q] += 1
        for k, v in deps.items():
            if k[0] == "eng":
                self.needed[k[1]].add(v - 1)
        self.ops[q].append(dict(fn=fn, deps=deps, seq=seq, dma=(slot, val)))
        self._commit(evkey, val, reads, writes)
        return evkey, val

    def cc(self, fn, reads=(), writes=()):
        q = "pool"
        deps = self._collect(q, reads, writes)
        i = self.dma_count.get("cc", 0)
        self.dma_count["cc"] = i + 1
        evkey = ("cc",)
        val = i + 1
        seq = self.nseq[q]
        self.nseq[q] += 1
        for k, v in deps.items():
            if k[0] == "eng":
                self.needed[k[1]].add(v - 1)
        self.ops[q].append(dict(fn=fn, deps=deps, seq=seq, dma=("cc", val)))
        self._commit(evkey, val, reads, writes)

    def barrier(self):
        deps = {}
        for q in ("sp", "pool", "act"):
            n = self.dma_count[q]
            for slot in range(min(n, self.ring)):
                cnt = (n - 1 - slot) // self.ring + 1
                deps[("dma", q, slot)] = 16 * cnt
        for e in ENGS:
            last = None
            for o in reversed(self.ops[e]):
                if o["dma"] is None and o["fn"] is not None:
                    last = o["seq"]
                    break
            if last is not None:
                deps[("eng", e)] = last + 1
                self.needed[e].add(last)
        for e in ENGS:
            d = {k: v for k, v in deps.items() if k != ("eng", e)}
            seq = self.nseq[e]
            self.nseq[e] += 1
            self.ops[e].append(dict(fn=None, deps=d, seq=seq, dma=None))

    def final_wait_all(self, eng="sp"):
        deps = {}
        for q in ("sp", "pool", "act"):
            n = self.dma_count[q]
            for slot in range(min(n, self.ring)):
                cnt = (n - 1 - slot) // self.ring + 1
                deps[("dma", q, slot)] = 16 * cnt
        for e in ENGS:
            if e != eng and self.nseq[e] > 0:
                last = None
                for o in reversed(self.ops[e]):
                    if o["dma"] is None and o["fn"] is not None:
                        last = o["seq"]
                        break
                if last is not None:
                    deps[("eng", e)] = last + 1
                    self.needed[e].add(last)
        if self.dma_count.get("cc", 0):
            deps[("cc",)] = self.dma_count["cc"]
        seq = self.nseq[eng]
        self.nseq[eng] += 1
        self.ops[eng].append(dict(fn=None, deps=deps, seq=seq, dma=None))

    def emit(self, block, stack):
        nc = self.nc
        sems = {}

        def sem(name):
            if name not in sems:
                sems[name] = stack.enter_context(nc.semaphore(name))
            return sems[name]
        cum = {}
        for e in ENGS:
            c = 0
            m = {}
            for o in self.ops[e]:
                if o["dma"] is None and o["seq"] in self.needed[e]:
                    c += 1
                    m[o["seq"]] = c
            cum[e] = m
        for e in ENGS:
            sem("s_" + e)
        for q in ("sp", "pool", "act"):
            for s in range(min(self.dma_count[q], self.ring)):
                sem("d_%s_%d" % (q, s))
        if self.dma_count.get("cc", 0):
            sem("s_cc")

        def run(e, engine):
            waited = {}
            for o in self.ops[e]:
                for k, v in o["deps"].items():
                    if k[0] == "eng":
                        s = sems["s_" + k[1]]
                        tv = cum[k[1]][v - 1]
                    elif k[0] == "dma":
                        s = sems["d_%s_%d" % (k[1], k[2])]
                        tv = v
                    else:
                        s = sems["s_cc"]
                        tv = v
                    if waited.get(k, 0) < tv:
                        engine.wait_ge(s, tv)
                        waited[k] = tv
                if o["fn"] is None:
                    continue
                ins = o["fn"](engine)
                if o["dma"] is not None:
                    if o["dma"][0] == "cc":
                        ins.then_inc(sems["s_cc"], 1)
                    else:
                        ins.then_inc(sems["d_%s_%d" % (e, o["dma"][0])], 16)
                elif o["seq"] in self.needed[e]:
                    ins.then_inc(sems["s_" + e], 1)

        if self.ops["pe"]:
            @block.tensor
            def _(eng):
                run("pe", eng)
        if self.ops["act"]:
            @block.scalar
            def _(eng):
                run("act", eng)
        if self.ops["dve"]:
            @block.vector
            def _(eng):
                run("dve", eng)
        if self.ops["pool"]:
            @block.gpsimd
            def _(eng):
                run("pool", eng)
        if self.ops["sp"]:
            @block.sync
            def _(eng):
                run("sp", eng)

import numpy as np
from contextlib import ExitStack
from concourse.bass_utils import run_bass_kernel_spmd
import ml_dtypes

F32 = mybir.dt.float32
BF16 = mybir.dt.bfloat16
AF = mybir.ActivationFunctionType
ALU = mybir.AluOpType

TN = 2192
B0 = 64
HIST0 = 2112
S0 = 2128
GB = [0, 439, 878, 1317, 1755, 2192]
NGRP = 5
EPS = 1e-6
SB_BASE = 16512
SB_END = 229376


def build_program(dbg_stage=99, small=False, sub=0):
    nc = bass.Bass("TRN2", target_bir_lowering=False)

    def din(name, shape, dt=F32):
        return nc.dram_tensor(name, shape, dt, kind="ExternalInput").ap()

    def dout(name, shape, dt=F32):
        return nc.dram_tensor(name, shape, dt, kind="ExternalOutput").ap()

    xc = din("xc", [TN, 1024])
    cpool = din("cpool", [16, 1024])
    ck_in = din("ck_in", [2048, 1024])
    cv_in = din("cv_in", [2048, 1024])
    clf = din("clf", [128, 256])
    win = din("win", [4, 22, 128, 2048] if not small else [1, 1, 128, 2048])
    wout = din("wout", [4, 2, 8, 128, 1408] if not small else [1, 1, 1, 128, 1408])
    wpool = din("wpool", [128, 2048])
    wkv = din("wkv", [128, 8 * 2048])
    wfg = din("wfg", [128, 128])
    wq = din("wq", [16, 128, 512])
    wo = din("wo", [16, 64, 1024])
    gall = din("gall", [128, 72])
    bfg = din("bfg", [128, 16])
    ident_in = din("ident", [128, 128])
    tri_in = din("tri", [128, 128])
    invc_in = din("invc", [128, 64])
    mrank_in = din("mrank", [128, 4])
    pen_in = din("pen", [16, 128, 512])
    pens_in = din("pens", [128, 64])

    y_p = dout("y_p", [2048, 1024])
    y_s = dout("y_s", [64, 1024])
    pool_p = dout("pool_p", [16, 1024])
    pool_s = dout("pool_s", [16, 1024])
    k_p = dout("k_p", [2048, 1024])
    v_p = dout("v_p", [2048, 1024])
    lf_p = dout("lf_p", [2048, 16])
    k_s = dout("k_s", [64, 1024])
    v_s = dout("v_s", [64, 1024])
    lf_s = dout("lf_s", [64, 16])

    KTx = [nc.dram_tensor("KTx%d" % i, [128, 4096], BF16) for i in range(4)]
    KTall = [nc.dram_tensor("KTall%d" % i, [512, 4096], BF16) for i in range(4)]
    Vx = [nc.dram_tensor("Vx%d" % i, [128, 2560], BF16) for i in range(8)]
    Vall = [nc.dram_tensor("Vall%d" % i, [512, 2560], BF16) for i in range(8)]
    KTxv = [t.ap().rearrange("p (a c) -> (p a) c", a=2) for t in KTx]
    Vxv = [t.ap().rearrange("p (a c) -> (p a) c", a=2).rearrange("(h p) (l c) -> p h l c", p=128, c=80) for t in Vx]
    cw_x = nc.dram_tensor("cw_x", [128, 256], F32)
    cw_all = nc.dram_tensor("cw_all", [512, 256], F32)
    cq_d = nc.dram_tensor("cq_d", [16, 2112], BF16)

    S = Sched(nc, ring=8)
    GBm = list(GB)

    cur = [SB_BASE]

    def alloc(name, shape, dt, at=None):
        nbytes = int(np.prod(shape[1:])) * (4 if dt == F32 else 2)
        nbytes = (nbytes + 31) // 32 * 32
        if at is None:
            off = cur[0]
            cur[0] += nbytes
        else:
            off = at
        assert off + nbytes <= SB_END, (name, off, nbytes)
        return nc.alloc_sbuf_tensor_at(name, shape, dt, offset=off), off + nbytes

    xT, _ = alloc("xT", [128, 8, TN], F32)
    gal, _ = alloc("gal", [128, 72], F32)
    ident, _ = alloc("ident", [128, 128], F32)
    identb, _ = alloc("identb", [128, 128], BF16)
    tri, _ = alloc("tri", [128, 128], F32)
    onesb, _ = alloc("onesb", [128, 128], BF16)
    onesf, _ = alloc("onesf", [128, 128], F32)
    bfg_t, _ = alloc("bfg_t", [128, 16], F32)
    invc, _ = alloc("invc", [128, 4, 16], F32)
    mrank, _ = alloc("mrank", [128, 4], F32)
    epsb, _ = alloc("epsb", [128, 1], F32)
    one1, _ = alloc("one1", [128, 1], F32)
    KTs, _ = alloc("KTs", [64, 16, 64], BF16)
    Vsn, _ = alloc("Vsn", [128, 1024], BF16)
    lfall, _ = alloc("lfall", [128, 17, 16], F32)
    cwt, _ = alloc("cwt", [128, 17, 16], F32)
    cq, _ = alloc("cq", [128, 17, 16], F32)
    ncks, _ = alloc("ncks", [128, 17, 16], F32)
    upT, _ = alloc("upT", [128, 8, 32], F32)
    xnT, _ = alloc("xnT", [128, 8, TN], BF16)
    RSTD_OFF = cur[0]
    rstd, _ = alloc("rstd", [128, TN], F32)
    OV = cur[0]
    o = OV
    actT, o = alloc("actT", [128, 11, TN], BF16, at=o)
    win_t = []
    for i in range(3):
        t, o = alloc("win_t%d" % i, [128, 2, 8, 128], BF16, at=o)
        win_t.append(t)
    wout_t = []
    for i in range(2):
        t, o = alloc("wout_t%d" % i, [128, 11, 128], BF16, at=o)
        wout_t.append(t)
    wst = []
    for i in range(3):
        t, o = alloc("wst%d" % i, [128, 1024], F32, at=o)
        wst.append(t)
    sq = []
    for i in range(2):
        t, o = alloc("sq%d" % i, [128, 512], BF16, at=o)
        sq.append(t)
    sil = []
    for i in range(2):
        t, o = alloc("sil%d" % i, [128, 512], F32, at=o)
        sil.append(t)
    FFN_END = o
    o = OV
    xin = []
    for i in range(2):
        t, o = alloc("xin%d" % i, [128, 1024], F32, at=o)
        xin.append(t)
    o = OV
    ut, o = alloc("ut", [128, TN], F32, at=o)
    s1, o = alloc("s1", [128, TN], F32, at=o)
    s2, o = alloc("s2", [128, TN], F32, at=o)
    wpool_t, o = alloc("wpool_t", [128, 4, 2, 256], BF16, at=o)
    hist_in, o = alloc("hist_in", [16, 1024], F32, at=o)
    histT, o = alloc("histT", [128, 8, 16], F32, at=o)
    tmp16, o = alloc("tmp16", [128, 16], F32, at=o)
    pout, o = alloc("pout", [32, 1024], F32, at=o)
    ut_b, o = alloc("ut_b", [128, TN], F32, at=o)
    s1_b, o = alloc("s1_b", [128, TN], F32, at=o)
    s2_b, o = alloc("s2_b", [128, TN], F32, at=o)
    tmp16_b, o = alloc("tmp16_b", [128, 16], F32, at=o)
    o = OV
    wkv_t, o = alloc("wkv_t", [128, 8, 2048], BF16, at=o)
    wfg_t, o = alloc("wfg_t", [128, 8, 16], BF16, at=o)
    stage = []
    for i in range(2):
        t, o = alloc("stage%d" % i, [128, 2048], F32, at=o)
        stage.append(t)
    vst = []
    for i in range(2):
        t, o = alloc("vst%d" % i, [128, 16, 80], BF16, at=o)
        vst.append(t)
    kst = []
    for i in range(4):
        t, o = alloc("kst%d" % i, [128, 512], BF16, at=o)
        kst.append(t)
    zt = []
    for i in range(5):
        t, o = alloc("zt%d" % i, [128, 16], F32, at=o)
        zt.append(t)
    clf_t, o = alloc("clf_t", [128, 16, 16], F32, at=o)
    cwp, o = alloc("cwp", [128, 16, 16], F32, at=o)
    sc1, o = alloc("sc1", [128, 16, 16], F32, at=o)
    sc2, o = alloc("sc2", [128, 16, 16], F32, at=o)
    tots, o = alloc("tots", [128, 16, 16], F32, at=o)
    o = OV
    KTh, o = alloc("KTh", [128, 8192], BF16, at=o)
    Vh, o = alloc("Vh", [128, 64, 80], BF16, at=o)
    Qh, o = alloc("Qh", [128, 2112], BF16, at=o)
    pen_t, o = alloc("pen_t", [128, 16, 512], BF16, at=o)
    pens_t, o = alloc("pens_t", [128, 64], BF16, at=o)
    PT = []
    for i in range(4):
        t, o = alloc("PT%d" % i, [128, 512], BF16, at=o)
        PT.append(t)
    nck, o = alloc("nck", [128, 64, 16], F32, at=o)
    cwg, o = alloc("cwg", [128, 64, 16], F32, at=o)
    totb, o = alloc("totb", [128, 64, 16], F32, at=o)
    scA, o = alloc("scA", [128, 64, 16], F32, at=o)
    rd, o = alloc("rd", [65, 512], F32, at=o)
    rbs, o = alloc("rbs", [64, 512], F32, at=o)
    OTn, o = alloc("OTn", [64, 512], BF16, at=o)
    wq_t, o = alloc("wq_t", [128, 8, 64], BF16, at=o)
    wo_t, o = alloc("wo_t", [64, 1024], BF16, at=o)
    ckraw, o = alloc("ckraw", [128, 16, 64], BF16, at=o)
    cvh, o = alloc("cvh", [128, 16, 64], BF16, at=o)
    KTsh, o = alloc("KTsh", [128, 2112], BF16, at=o)
    PTs = []
    for i in range(2):
        t, o = alloc("PTs%d" % i, [128, 64], BF16, at=o)
        PTs.append(t)
    rs, o = alloc("rs", [64, 64], F32, at=o)
    OTs, o = alloc("OTs", [64, 64], BF16, at=o)
    cqT, _ = alloc("cqT", [16, 2112], BF16, at=RSTD_OFF)
    o = OV
    ytmp = []
    for i in range(2):
        t, o = alloc("ytmp%d" % i, [128, 128], F32, at=o)
        ytmp.append(t)
    ostage = []
    for i in range(2):
        t, o = alloc("ostage%d" % i, [128, 1024], F32, at=o)
        ostage.append(t)

    ps = nc.alloc_psum_tensor("ps", [128, 8, 512], F32)

    def bank(i):
        return ps[:, i, :]

    def P(i):
        return ("ps", i)

    def mm(out, lhsT, rhs, start, stop, reads, writes):
        S.op("pe", lambda e: e.matmul(out, lhsT=lhsT, rhs=rhs, start=start, stop=stop), reads, writes)

    def tr(out, in_, idn, reads, writes):
        S.op("pe", lambda e: e.transpose(out=out, in_=in_, identity=idn), reads, writes)

    def act(out, in_, func, reads, writes, bias=None, scale=1.0):
        if bias is None:
            S.op("act", lambda e: e.activation(out=out, in_=in_, func=func, scale=scale), reads, writes)
        else:
            S.op("act", lambda e: e.activation(out=out, in_=in_, func=func, bias=bias, scale=scale), reads, writes)

    def cp(eng, out, in_, reads, writes):
        if eng == "act":
            S.op("act", lambda e: e.copy(out=out, in_=in_), reads, writes)
        else:
            S.op(eng, lambda e: e.tensor_copy(out=out, in_=in_), reads, writes)

    def stt(eng, out, in0, scalar, in1, op0, op1, reads, writes):
        S.op(eng, lambda e: e.scalar_tensor_tensor(out=out, in0=in0, scalar=scalar, in1=in1, op0=op0, op1=op1),
             reads, writes)

    def tt(eng, out, in0, in1, op, reads, writes):
        S.op(eng, lambda e: e.tensor_tensor(out=out, in0=in0, in1=in1, op=op), reads, writes)

    def ts(eng, out, in0, s1_, s2_, op0, op1, reads, writes):
        S.op(eng, lambda e: e.tensor_scalar(out=out, in0=in0, scalar1=s1_, scalar2=s2_, op0=op0, op1=op1),
             reads, writes)

    def tss(eng, out, in0, s1_, op0, reads, writes):
        S.op(eng, lambda e: e.tensor_single_scalar(out=out, in_=in0, scalar=s1_, op=op0), reads, writes)

    def memset(eng, ap, val, writes):
        S.op(eng, lambda e: e.memset(ap, val), (), writes)

    def dma(q, out, in_, reads, writes):
        S.dma(q, lambda e: e.dma_start(out=out, in_=in_), reads, writes)

    XT = [("xT", k) for k in range(8)]
    XN = [("xn", k) for k in range(8)]

    dma("sp", gal[:, :], gall[:, :], (), ["gal"])
    dma("sp", ident[:, :], ident_in[:, :], (), ["ident"])
    dma("sp", tri[:, :], tri_in[:, :], (), ["tri"])
    dma("sp", bfg_t[:, :], bfg[:, :], (), ["bfg"])
    dma("sp", invc[:, :, :], invc_in.rearrange("p (g c) -> p g c", g=4), (), ["invc"])
    dma("sp", mrank[:, :], mrank_in[:, :], (), ["mrank"])
    memset("dve", onesb[:, :], 1.0, ["onesb"])
    memset("dve", onesf[:, :], 1.0, ["onesf"])
    memset("dve", epsb[:, :], EPS, ["epsb"])
    memset("dve", one1[:, :], 1.0, ["one1"])
    memset("dve", lfall[:, :, :], 0.0, ["lfall"])
    cp("dve", identb[:, :], ident[:, :], ["ident"], ["identb"])

    for t in range(18):
        r0 = 128 * t
        nr = min(128, TN - r0)
        xb = xin[t % 2]
        kx = ("xin", t % 2)
        dma("sp", xb[0:nr, :], xc[r0:r0 + nr, :], (), [kx])
        for half in range(2):
            bk = 6 + half
            for i in range(4):
                c = 4 * half + i
                tr(bank(bk)[:, i * 128:i * 128 + nr], xb[0:nr, c * 128:(c + 1) * 128], ident[0:nr, 0:nr],
                   [kx, "ident"], [P(bk)])
            src = bank(bk).rearrange("p (a b) -> p a b", a=4)[:, :, 0:nr]
            cp("act" if half == 0 else "dve", xT[:, 4 * half:4 * half + 4, r0:r0 + nr], src, [P(bk)],
               [("xT", 4 * half + i) for i in range(4)])

    S.barrier()
    def norm_stats_g(g):
        c0, c1 = GBm[g], GBm[g + 1]
        n = c1 - c0
        for kc in range(8):
            sb_ = sq[kc % 2]
            act(sb_[:, 0:n], xT[:, kc, c0:c1], AF.Square, [("xT", kc)], [("sq", kc % 2)])
            mm(bank(6)[:, 0:n], onesb[:, :], sb_[:, 0:n], kc == 0, kc == 7, [("sq", kc % 2), "onesb"], [P(6)])
        act(rstd[:, c0:c1], bank(6)[:, 0:n], AF.Sqrt, [P(6), "epsb"], [("rstd", g)], bias=epsb[:, 0:1], scale=1.0 / 1024)
        S.op("dve", lambda e, a=rstd[:, c0:c1]: e.reciprocal(out=a, in_=a), [("rstd", g)], [("rstd", g), "rstd"])

    def norm_stats():
        for g in range(NGRP):
            norm_stats_g(g)

    def norm_apply_g(gi, g):
        c0, c1 = GBm[g], GBm[g + 1]
        for kc in range(8):
            stt("dve", xnT[:, kc, c0:c1], xT[:, kc, c0:c1], gal[:, gi * 8 + kc:gi * 8 + kc + 1], rstd[:, c0:c1],
                ALU.mult, ALU.mult, [("xT", kc), ("rstd", g), "gal"], [("xn", kc)])

    def norm_apply(gi):
        pass

    def norm_full(gi):
        for g in range(NGRP):
            norm_stats_g(g)
            norm_apply_g(gi, g)

    wcnt = {"in": 0, "out": 0, "st": 0}

    def wload(dst3, src2, nk, wkey):
        sl = wcnt["st"] % 3
        wcnt["st"] += 1
        stg = wst[sl]
        dma("sp", stg[:, 0:nk * 128], src2, (), [("wst", sl)])
        cp("pool", dst3, stg[:, 0:nk * 128].rearrange("p (k n) -> p k n", k=nk), [("wst", sl)], [wkey])


    def ffn(f):
        for half in range(2):
            for ci in range(11):
                ffc = half * 11 + ci
                slot = wcnt["in"] % 3
                wcnt["in"] += 1
                wt = win_t[slot]
                wload(wt[:, 0, :, :], win[f, ffc][:, 0:1024], 8, ("win", slot))
                wload(wt[:, 1, :, :], win[f, ffc][:, 1024:2048], 8, ("win", slot))
                for g in range(NGRP):
                    c0, c1 = GBm[g], GBm[g + 1]
                    n = c1 - c0
                    bg = g % 2
                    bu = 2 + g % 2
                    for kc in range(8):
                        mm(bank(bg)[:, 0:n], wt[:, 0, kc, :], xnT[:, kc, c0:c1], kc == 0, kc == 7,
                           [("win", slot), ("xn", kc)], [P(bg)])
                    for kc in range(8):
                        mm(bank(bu)[:, 0:n], wt[:, 1, kc, :], xnT[:, kc, c0:c1], kc == 0, kc == 7,
                           [("win", slot), ("xn", kc)], [P(bu)])
                    st = sil[g % 2]
                    act(st[:, 0:n], bank(bg)[:, 0:n], AF.Silu, [P(bg)], [("sil", g % 2)])
                    tt("dve", actT[:, ci, c0:c1], st[:, 0:n], bank(bu)[:, 0:n], ALU.mult,
                       [("sil", g % 2), P(bu)], [("act", ci)])
            for oc in range(8):
                slot = wcnt["out"] % 2
                wcnt["out"] += 1
                wt = wout_t[slot]
                wload(wt[:, 0:8, :], wout[f, half, oc][:, 0:1024], 8, ("wout", slot))
                wload(wt[:, 8:11, :], wout[f, half, oc][:, 1024:1408], 3, ("wout", slot))
                for g in range(NGRP):
                    c0, c1 = GBm[g], GBm[g + 1]
                    n = c1 - c0
                    by = 4 + g % 2
                    for k in range(11):
                        mm(bank(by)[:, 0:n], wt[:, k, :], actT[:, k, c0:c1], k == 0, k == 10,
                           [("wout", slot), ("act", k)], [P(by)])
                    stt("dve", xT[:, oc, c0:c1], bank(by)[:, 0:n], 0.5, xT[:, oc, c0:c1], ALU.mult, ALU.add,
                        [P(by), ("xT", oc)], [("xT", oc)])

    def final_out(raw):
        S.barrier()
        if not raw:
            norm_stats()
        for (c0, nt, wi) in windows_f:
            og = ostage[wi % 2]
            ko = ("ostage", wi % 2)
            for kc in range(8):
                yt = ytmp[kc % 2]
                ky = ("ytmp", kc % 2)
                if raw:
                    cp("dve", yt[:, 0:nt], xT[:, kc, c0:c0 + nt], [("xT", kc)], [ky])
                else:
                    stt("dve", yt[:, 0:nt], xT[:, kc, c0:c0 + nt], gal[:, 56 + kc:57 + kc], rstd[:, c0:c0 + nt], ALU.mult, ALU.mult,
                        [("xT", kc), "rstd", "gal"], [ky])
                bk = 6 + kc // 4
                tr(bank(bk)[0:nt, (kc % 4) * 128:(kc % 4 + 1) * 128], yt[:, 0:nt], ident[:, :], [ky, "ident"], [P(bk)])
            cp("act", og[0:nt, 0:512], bank(6)[0:nt, :], [P(6)], [ko])
            cp("act", og[0:nt, 512:1024], bank(7)[0:nt, :], [P(7)], [ko])
            if wi < 16:
                dma("sp", y_p[wi * 128:(wi + 1) * 128, :], og[:, :], [ko], ["y_p"])
            else:
                dma("sp", y_s[:, :], og[0:64, :], [ko], ["y_s"])
        S.final_wait_all("sp")
        with ExitStack() as st:
            block = st.enter_context(nc.Block())
            S.emit(block, st)
        return nc

    windows_f = [(64 + 128 * lt, 128, lt) for lt in range(16)] + [(S0, 64, 16)]
    if dbg_stage == 0:
        return final_out(True)
    if small:
        def ffn(f):
            pass
    norm_full(0)
    ffn(0)

    if dbg_stage == 1:
        return final_out(True)
    S.barrier()
    norm_stats()
    dma("pool", wpool_t[:, :, :, :], wpool.rearrange("p (g k n) -> p g k n", g=4, k=2), (), ["wpool"])
    dma("sp", hist_in[:, :], cpool[:, :], (), ["hist_in"])
    for kc in range(8):
        tr(bank(7)[:, kc * 16:(kc + 1) * 16], hist_in[0:16, kc * 128:(kc + 1) * 128], ident[0:16, 0:16],
           ["hist_in", "ident"], [P(7)])
    cp("act", histT[:, :, :], bank(7)[:, 0:128].rearrange("p (a b) -> p a b", a=8), [P(7)], ["histT"])

    def useg(t_):
        v = t_[:, 0:2112].rearrange("p (m c) -> p m c", c=528)
        return v[:, :, 0:16], v[:, :, 16:528]

    for kc in range(8):
        grp = kc // 2
        w = 2 ** (grp + 1)
        par = kc % 2
        ut_, s1_, s2_, t16_ = (ut, s1, s2, tmp16) if par == 0 else (ut_b, s1_b, s2_b, tmp16_b)
        ku, k1, k2, kt16 = ("ut", par), ("s1", par), ("s2", par), ("tmp16", par)
        weng = "dve" if par == 0 else "pool"
        gsc = gal[:, 8 + kc:8 + kc + 1]
        uh, ub = useg(ut_)
        rh = rstd[:, 0:64].rearrange("p (m c) -> p m c", c=16)
        rb = rstd[:, 64:2112].rearrange("p (m c) -> p m c", c=512)
        xh = xT[:, kc, 0:64].rearrange("p (m c) -> p m c", c=16)
        xbk = xT[:, kc, 64:2112].rearrange("p (m c) -> p m c", c=512)
        stt("dve", uh, xh, gsc, rh, ALU.mult, ALU.mult, [("xT", kc), "rstd", "gal"], [ku])
        stt("dve", ub, xbk, gsc, rb, ALU.mult, ALU.mult, [("xT", kc), "rstd", "gal"], [ku])
        stt("dve", ut_[:, S0:TN], xT[:, kc, S0:TN], gsc, rstd[:, S0:TN], ALU.mult, ALU.mult,
            [("xT", kc), "rstd", "gal"], [ku])
        cp("pool", ut_[:, HIST0:S0], histT[:, kc, :], ["histT"], [ku])
        cp("act", upT[:, kc, 0:16], ut_[:, 2096:2112], [ku], ["upT"])
        cp("act", upT[:, kc, 16:32], ut_[:, 2176:2192], [ku], ["upT"])
        prev, pk = ut_, ku
        pp = [(s1_, k1), (s2_, k2)]
        for j in range(grp + 1):
            sh = 2 ** j
            nt_, nk_ = pp[j % 2]
            cp(weng, nt_[:, 0:sh], prev[:, 0:sh], [pk], [nk_])
            tt(weng, nt_[:, sh:TN], prev[:, sh:TN], prev[:, 0:TN - sh], ALU.add, [pk], [nk_])
            prev, pk = nt_, nk_
        _, sbv = useg(prev)
        dxb = xnT[:, kc, 64:2112].rearrange("p (m c) -> p m c", c=512)
        stt("dve", dxb, sbv, 1.0 / w, ub, ALU.mult, ALU.subtract, [pk, ku], [("xn", kc)])
        stt("dve", xnT[:, kc, S0:TN], prev[:, S0:TN], 1.0 / w, ut_[:, S0:TN], ALU.mult, ALU.subtract,
            [pk, ku], [("xn", kc)])
        tt("dve", t16_[:, :], prev[:, 16:32], invc[:, grp, :], ALU.mult, [pk, "invc"], [kt16])
        tt("dve", xnT[:, kc, 64:80], t16_[:, :], ut_[:, 16:32], ALU.subtract, [kt16, ku, ("xn", kc)], [("xn", kc)])
    colgroups = [(64 + 512 * m, 64 + 512 * (m + 1)) for m in range(4)] + [(S0, TN)]
    ib = 0
    for (c0, c1) in colgroups:
        n = c1 - c0
        for grp in range(4):
            for oc2 in range(2):
                bk = ib % 2
                ib += 1
                for k2 in range(2):
                    mm(bank(bk)[:, 0:n], wpool_t[:, grp, k2, oc2 * 128:(oc2 + 1) * 128], xnT[:, 2 * grp + k2, c0:c1],
                       k2 == 0, k2 == 1, ["wpool", ("xn", 2 * grp + k2)], [P(bk)])
                oc = 2 * grp + oc2
                stt("dve", xT[:, oc, c0:c1], bank(bk)[:, 0:n], gal[:, 64 + oc:64 + oc + 1], xT[:, oc, c0:c1],
                    ALU.mult, ALU.add, [P(bk), ("xT", oc), "gal"], [("xT", oc)])
    for kc in range(8):
        bk = 6 + kc // 4
        tr(bank(bk)[0:32, (kc % 4) * 128:(kc % 4 + 1) * 128], upT[:, kc, :], ident[:, :], ["upT", "ident"], [P(bk)])
    cp("act", pout[:, 0:512], bank(6)[0:32, :], [P(6)], ["pout"])
    cp("act", pout[:, 512:1024], bank(7)[0:32, :], [P(7)], ["pout"])
    dma("sp", pool_p[:, :], pout[0:16, :], ["pout"], ["pool_p"])
    dma("sp", pool_s[:, :], pout[16:32, :], ["pout"], ["pool_s"])

    if dbg_stage == 2:
        return final_out(True)
    GBm[:] = [64, 490, 916, 1342, 1768, 2192]
    S.barrier()
    norm_full(2)
    ffn(1)

    if dbg_stage == 3:
        return final_out(True)
    S.barrier()
    norm_full(3)
    for kc_ in range(8):
        for hf in range(2):
            wload(wkv_t[:, kc_, hf * 1024:(hf + 1) * 1024].rearrange("p (k n) -> p k n", k=8),
                  wkv[:, kc_ * 2048 + hf * 1024:kc_ * 2048 + (hf + 1) * 1024], 8, "wkv")
    dma("pool", wfg_t[:, :, :], wfg.rearrange("p (k n) -> p k n", k=8), (), ["wfg"])
    for i in range(2):
        if sub & 4:
            continue
        memset("pool", vst[i][:, :, :], 0.0, [("vst", i)])
        memset("pool", vst[i][:, :, 64:65], 1.0, [("vst", i)])
    windows = [(64 + 128 * lt, 128, lt) for lt in range(16)] + [(S0, 64, 16)]
    for (c0, nt, wi) in windows:
        if sub & 8:
            continue
        if (sub & 64) and wi >= 2:
            continue
        sg = stage[wi % 2]
        ksg = ("stage", wi % 2)
        vt = vst[wi % 2]
        kvt = ("vst", wi % 2)
        for cb in range(4):
            bk = cb % 2
            for kc in range(8):
                mm(bank(bk)[0:nt, :], xnT[:, kc, c0:c0 + nt], wkv_t[:, kc, cb * 512:(cb + 1) * 512], kc == 0, kc == 7,
                   [("xn", kc), "wkv"], [P(bk)])
            cp("act", sg[0:nt, cb * 512:(cb + 1) * 512], bank(bk)[0:nt, :], [P(bk)], [ksg])
            if cb >= 2 and not (sub & 16):
                srcv = sg[0:nt, cb * 512:(cb + 1) * 512].rearrange("p (h d) -> p h d", d=64)
                if wi < 16:
                    cp("dve", vt[0:nt, (cb - 2) * 8:(cb - 1) * 8, 0:64], srcv, [ksg], [kvt])
                else:
                    cp("dve", Vsn[0:nt, (cb - 2) * 512:(cb - 1) * 512], sg[0:nt, cb * 512:(cb + 1) * 512], [ksg], ["Vsn"])
        if wi < 16:
            if not (sub & 32):
                dma("sp", k_p[wi * 128:(wi + 1) * 128, :], sg[:, 0:1024], [ksg], ["k_p"])
                dma("sp", v_p[wi * 128:(wi + 1) * 128, :], sg[:, 1024:2048], [ksg], ["v_p"])
            for j in range(8):
                if sub & 2:
                    continue
                dma("sp", Vxv[j][:, :, wi, :], vt[:, 2 * j:2 * j + 2, :], [kvt], [("V_x", j)])
        else:
            dma("sp", k_s[:, :], sg[0:64, 0:1024], [ksg], ["k_s"])
            dma("sp", v_s[:, :], sg[0:64, 1024:2048], [ksg], ["v_s"])
        if sub & 1:
            continue
        for kc in range(8):
            mm(bank(2)[0:nt, 0:16], xnT[:, kc, c0:c0 + nt], wfg_t[:, kc, :], kc == 0, kc == 7, [("xn", kc), "wfg"], [P(2)])
        z, nz, e_, l_, m_ = [zt[i][0:nt, :] for i in range(5)]
        tt("dve", z, bank(2)[0:nt, 0:16], bfg_t[0:nt, :], ALU.add, [P(2), "bfg"], ["z0"])
        S.op("dve", lambda e, a=nz, b=z: e.tensor_scalar_mul(out=a, in0=b, scalar1=-1.0), ["z0"], ["z1"])
        tt("dve", nz, z, nz, ALU.min, ["z0", "z1"], ["z1"])
        act(e_, nz, AF.Exp, ["z1"], ["z2"])
        act(l_, e_, AF.Ln, ["z2", "one1"], ["z3"], bias=one1[0:nt, 0:1])
        S.op("dve", lambda e, a=m_, b=z: e.tensor_scalar_min(out=a, in0=b, scalar1=0.0), ["z0"], ["z4"])
        tt("dve", lfall[0:nt, wi, :], m_, l_, ALU.subtract, ["z4", "z3"], ["lfall"])
    for wi_ in range(16):
        dma("sp", lf_p[wi_ * 128:(wi_ + 1) * 128, :], lfall[:, wi_, :], ["lfall"], ["lf_p"])
    dma("sp", lf_s[:, :], lfall[0:64, 16, :], ["lfall"], ["lf_s"])
    if dbg_stage == 31:
        return final_out(True)
    ik = 0
    for hp in range(8):
        for gi_, (c0, c1) in enumerate(colgroups[0:4]):
            n = c1 - c0
            bk = ik % 2
            kb = kst[ik % 4]
            kk = ("kst", ik % 4)
            ik += 1
            for kc in range(8):
                mm(bank(bk)[:, 0:n], wkv_t[:, kc, hp * 128:(hp + 1) * 128], xnT[:, kc, c0:c1], kc == 0, kc == 7,
                   ["wkv", ("xn", kc)], [P(bk)])
            cp("act" if ik % 2 else "dve", kb[:, 0:n], bank(bk)[:, 0:n], [P(bk)], [kk])
            i4 = (2 * hp) // 4
            r0_ = ((2 * hp) % 4) * 64
            dma("sp", KTxv[i4][r0_:r0_ + 128, gi_ * 512:(gi_ + 1) * 512], kb[:, 0:n], [kk], [("KT_x", i4)])
    (c0, c1) = colgroups[4]
    for h in range(16):
        bk = 2 + h % 2
        for kc in range(8):
            mm(bank(bk)[0:64, 0:64], wkv_t[:, kc, h * 64:(h + 1) * 64], xnT[:, kc, c0:c1], kc == 0, kc == 7,
               ["wkv", ("xn", kc)], [P(bk)])
        cp("dve", KTs[:, h, :], bank(bk)[0:64, 0:64], [P(bk)], ["KTs"])
    if dbg_stage == 32:
        return final_out(True)
    mm(bank(3)[:, 0:272], tri[:, :], lfall[:, :, :].rearrange("p a b -> p (a b)"), True, True, ["tri", "lfall"], [P(3)])
    cp("dve", cwt[:, :, :].rearrange("p a b -> p (a b)"), bank(3)[:, 0:272], [P(3)], ["cwt"])
    dma("sp", cw_x.ap()[:, :], cwt[:, 0:16, :].rearrange("p a b -> p (a b)"), ["cwt"], ["cw_x"])
    if dbg_stage == 33:
        return final_out(True)
    RG = [[0, 1, 2, 3], [4, 5, 6, 7]]
    S.cc(lambda e: e.collective_compute("AllGather", ALU.bypass, replica_groups=RG, ins=[cw_x.ap().opt()],
                                        outs=[cw_all.ap().opt()]), ["cw_x"], ["cw_all"])
    for i in range(4):
        S.cc(lambda e, a=KTx[i], c=KTall[i]: e.collective_compute("AllGather", ALU.bypass, replica_groups=RG, ins=[a.ap().opt()],
                                                                  outs=[c.ap().opt()]), [("KT_x", i)], [("KT_all", i)])
    for j in range(8):
        S.cc(lambda e, a=Vx[j], c=Vall[j]: e.collective_compute("AllGather", ALU.bypass, replica_groups=RG, ins=[a.ap().opt()],
                                                                outs=[c.ap().opt()]), [("V_x", j)], [("V_all", j)])
    if dbg_stage == 34:
        return final_out(True)
    dma("sp", clf_t[:, :, :], clf.rearrange("p (l h) -> p l h", l=16), (), ["clf"])
    clf2 = clf_t[:, :, :].rearrange("p a b -> p (a b)")
    mm(bank(4)[:, 0:256], tri[:, :], clf2, True, True, ["tri", "clf"], [P(4)])
    mm(bank(5)[:, 0:256], onesf[:, :], clf2, True, True, ["onesf", "clf"], [P(5)])
    cp("dve", cwp[:, :, :].rearrange("p a b -> p (a b)"), bank(4)[:, 0:256], [P(4)], ["cwp"])
    cp("dve", tots[:, :, :].rearrange("p a b -> p (a b)"), bank(5)[:, 0:256], [P(5)], ["tots"])
    prev, pk = tots, "tots"
    pp = [(sc1, "sc1"), (sc2, "sc2")]
    for j in range(4):
        sh = 2 ** j
        nt_, nk_ = pp[j % 2]
        cp("dve", nt_[:, 0:sh, :], prev[:, 0:sh, :], [pk], [nk_])
        tt("dve", nt_[:, sh:16, :], prev[:, sh:16, :], prev[:, 0:16 - sh, :], ALU.add, [pk], [nk_])
        prev, pk = nt_, nk_
    incl, ik_ = prev, pk
    tt("dve", cq[:, 0:16, :], incl[:, :, :], tots[:, :, :], ALU.subtract, [ik_, "tots"], ["cqtmp"])
    tt("dve", cq[:, 0:16, :], cq[:, 0:16, :], cwp[:, :, :], ALU.add, ["cqtmp", "cwp"], ["cqtmp"])
    S.op("dve", lambda e: e.tensor_scalar_mul(out=ncks[:, 0:16, :], in0=cq[:, 0:16, :], scalar1=-1.0), ["cqtmp"], ["ncks"])
    tt("dve", cq[:, 16, :], cwt[:, 16, :], incl[:, 15, :], ALU.add, ["cwt", ik_, "cqtmp"], ["cq"])
    S.op("dve", lambda e: e.tensor_scalar_mul(out=ncks[:, 16, :], in0=cq[:, 16, :], scalar1=-1.0), ["cq"], ["ncks"])

    if dbg_stage == 4:
        return final_out(True)
    S.barrier()
    norm_full(4)
    ffn(2)
    norm_full(5)
    S.barrier()

    if dbg_stage == 5:
        return final_out(True)
    cwav = cw_all.ap().rearrange("(r p) (m c) -> r p m c", p=128, m=4)
    cwgv = cwg[:, :, :].rearrange("p (m r w) h -> p m r (w h)", m=4, r=4)
    totv = totb[:, :, :].rearrange("p (m r w) h -> p m r (w h)", m=4, r=4)
    for r in range(4):
        dma("sp", cwgv[:, :, r, :], cwav[r], ["cw_all"], ["cwg"])
        for m in range(4):
            dma("sp", totv[:, m, r, :], cw_all.ap()[r * 128 + 127:r * 128 + 128, m * 64:(m + 1) * 64].to_broadcast([128, 64]),
                ["cw_all"], ["totb"])
    prev, pk = totb, "totb"
    pp = [(scA, "scA"), (nck, "nck")]
    for j in range(6):
        sh = 2 ** j
        nt_, nk_ = pp[j % 2]
        cp("dve", nt_[:, 0:sh, :], prev[:, 0:sh, :], [pk], [nk_])
        tt("dve", nt_[:, sh:64, :], prev[:, sh:64, :], prev[:, 0:64 - sh, :], ALU.add, [pk], [nk_])
        prev, pk = nt_, nk_
    tt("dve", scA[:, :, :], nck[:, :, :], totb[:, :, :], ALU.subtract, ["nck", "totb"], ["scA"])
    tt("dve", scA[:, :, :], scA[:, :, :], cwg[:, :, :], ALU.add, ["scA", "cwg"], ["scA"])
    S.op("dve", lambda e: e.tensor_scalar_mul(out=nck[:, :, :], in0=scA[:, :, :], scalar1=-1.0), ["scA"], ["nck"])
    ckv = scA[:, :, :].rearrange("p (m r w) h -> p m r (w h)", m=4, r=4)
    cqv = cq[:, 0:16, :].rearrange("p (m w) h -> p m (w h)", m=4)
    S.op("dve", lambda e: e.tensor_scalar_mul(out=cqv, in0=ckv[:, :, 0, :], scalar1=mrank[:, 0:1]), ["scA", "mrank", "cq"], ["cq"])
    for r in range(1, 4):
        stt("dve", cqv, ckv[:, :, r, :], mrank[:, r:r + 1], cqv, ALU.mult, ALU.add, ["scA", "mrank", "cq"], ["cq"])
    for rnd in range(5):
        nl = 4 if rnd < 4 else 1
        for i in range(nl):
            lt = rnd * 4 + i
            nt = 128 if lt < 16 else 64
            tr(bank(7)[0:16, i * 128:i * 128 + nt], cq[0:nt, lt, :], ident[0:nt, 0:nt], ["cq", "ident"], [P(7)])
        ncols = 512 if rnd < 4 else 64
        cp("act", cqT[:, rnd * 512:rnd * 512 + ncols], bank(7)[0:16, 0:ncols], [P(7)], ["cqT"])
    dma("sp", cq_d.ap()[:, :], cqT[:, :], ["cqT"], ["cq_d"])

    dma("pool", pen_t[:, :, :], pen_in.rearrange("k p j -> p k j"), (), ["pen"])
    dma("pool", pens_t[:, :], pens_in[:, :], (), ["pens"])
    memset("dve", KTh[64:128, :], 0.0, ["KTh"])
    memset("dve", KTsh[64:128, :], 0.0, ["KTsh"])
    memset("dve", Qh[64:128, :], 0.0, ["Qh"])
    memset("dve", KTh[64:65, :], 1.0, ["KTh"])
    memset("dve", KTsh[64:65, :], 1.0, ["KTsh"])
    KTav = [t.ap().rearrange("p (a c) -> (p a) c", a=2).rearrange("(r q) (m c) -> r q m c", r=4, m=4) for t in KTall]
    KThv = KTh[0:64, :].rearrange("p (m r c) -> p m r c", m=4, r=4)
    Vav = [t.ap().rearrange("p (a c) -> (p a) c", a=2).rearrange("(r q) (m c) -> r q m c", r=4, m=4) for t in Vall]
    Vhv = Vh[:, :, :].rearrange("p (m r w) c -> p m r (w c)", m=4, r=4)
    ckv_in = ck_in.rearrange("(l p) (h d) -> p l h d", p=128, d=64)
    cvv_in = cv_in.rearrange("(l p) (h d) -> p l h d", p=128, d=64)
    for h in range(16):
        dma("pool", wq_t[:, :, :], wq[h].rearrange("p (k n) -> p k n", k=8), (), ["wq"])
        dma("pool", wo_t[:, :], wo[h], (), ["wo"])
        for r in range(4):
            dma("sp", KThv[:, :, r, :], KTav[h // 4][r, (h % 4) * 64:(h % 4 + 1) * 64], [("KT_all", h // 4)], ["KTh"])
            dma("sp", Vhv[:, :, r, :], Vav[h // 2][r, (h % 2) * 128:(h % 2 + 1) * 128], [("V_all", h // 2)], ["Vh"])
        dma("pool", ckraw[:, :, :], ckv_in[:, :, h, :], (), ["ckraw"])
        dma("pool", cvh[:, :, :], cvv_in[:, :, h, :], (), ["cvh"])
        for gi_, (c0, c1) in enumerate(colgroups):
            n = c1 - c0
            for kc in range(8):
                mm(bank(7)[0:64, 0:n], wq_t[:, kc, :], xnT[:, kc, c0:c1], kc == 0, kc == 7, ["wq", ("xn", kc)], [P(7)])
            S.op("act", lambda e, a=Qh[0:64, gi_ * 512:gi_ * 512 + n], b=bank(7)[0:64, 0:n]: e.mul(out=a, in_=b, mul=0.125),
                 [P(7)], ["Qh"])
        dma("sp", Qh[64:65, :], cq_d.ap()[h:h + 1, :], ["cq_d"], ["Qh"])
        it = 0
        pending = []
        for m in range(4):
            nkt = 16 * (m + 1)
            qcols = Qh[:, m * 512:(m + 1) * 512]
            bo = 2 + m % 2
            SB_ = (0, 1, 5, 7)

            def s_mm(kt):
                bs_ = SB_[kt % 4]
                diag_ = kt >= 16 * m
                mm(bank(bs_), KTh[:, kt * 128:(kt + 1) * 128], qcols, True, not diag_, ["KTh", "Qh"], [P(bs_)])
                if diag_:
                    mm(bank(bs_), identb[:, :], pen_t[:, kt - 16 * m, :], False, True, ["identb", "pen"], [P(bs_)])
            for k0 in range(4):
                s_mm(k0)
            for j in range(nkt // 2):
                ka, kb = 2 * j, 2 * j + 1
                act(PT[ka % 4][:, :], bank(SB_[ka % 4]), AF.Exp, [P(SB_[ka % 4]), P(SB_[kb % 4]), "nck"], [("PT", ka % 4)],
                    bias=nck[:, ka, h:h + 1])
                act(PT[kb % 4][:, :], bank(SB_[kb % 4]), AF.Exp, [P(SB_[kb % 4]), "nck"], [("PT", kb % 4)],
                    bias=nck[:, kb, h:h + 1])
                mm(bank(bo)[0:65, :], Vh[:, ka, 0:65], PT[ka % 4][:, :], ka == 0, False,
                   ["Vh", ("PT", ka % 4), ("PT", kb % 4)], [P(bo)])
                mm(bank(bo)[0:65, :], Vh[:, kb, 0:65], PT[kb % 4][:, :], False, kb == nkt - 1,
                   ["Vh", ("PT", kb % 4)], [P(bo)])
                if ka + 4 < nkt:
                    s_mm(ka + 4)
                    s_mm(kb + 4)
                if j >= 2 and pending:
                    pending.pop(0)()

            def make_epi(m_=m, bo_=bo):
                def st_a():
                    cp("act", rd[64:65, :], bank(bo_)[64:65, :], [P(bo_)], ["rd"])
                    S.op("dve", lambda e: e.reciprocal(out=rd[64:65, :], in_=rd[64:65, :]), ["rd"], ["rd"])

                def st_b():
                    mm(bank(6)[0:64, :], onesf[64:65, 0:64], rd[64:65, :], True, True, ["onesf", "rd"], [P(6)])
                    cp("act", rbs[:, :], bank(6)[0:64, :], [P(6)], ["rbs"])
                    tt("dve", OTn[:, :], bank(bo_)[0:64, :], rbs[:, :], ALU.mult, [P(bo_), "rbs"], ["OTn"])

                def st_c(oc):
                    def f():
                        by = (4, 6)[oc % 2]
                        mm(bank(by), wo_t[:, oc * 128:(oc + 1) * 128], OTn[:, :], True, True, ["wo", "OTn"], [P(by)])
                        c0_ = 64 + 512 * m_
                        tt("dve", xT[:, oc, c0_:c0_ + 512], bank(by), xT[:, oc, c0_:c0_ + 512], ALU.add,
                           [P(by), ("xT", oc)], [("xT", oc)])
                    return f
                return [st_a, (lambda: None), (lambda: None), (lambda: None), st_b] + [st_c(oc) for oc in range(8)]
            assert not pending
            pending.extend(make_epi())
        while pending:
            pending.pop(0)()
        bb = bank(7).bitcast(BF16)
        for rnd in range(2):
            for i in range(8):
                lt = rnd * 8 + i
                tr(bb[0:64, i * 128:(i + 1) * 128], ckraw[:, lt, :], identb[:, :], ["ckraw", "identb"], [P(7)])
            cp("dve", KTsh[0:64, rnd * 1024:(rnd + 1) * 1024], bb[0:64, :], [P(7)], ["KTsh"])
        cp("dve", KTsh[0:64, 2048:2112], KTs[:, h, :], ["KTs"], ["KTsh"])
        qs = Qh[:, 2048:2112]
        def ss_mm(lt):
            nk_ = 128 if lt < 16 else 64
            bs_ = lt % 2
            mm(bank(bs_)[0:nk_, 0:64], KTsh[:, lt * 128:lt * 128 + nk_], qs, True, lt < 16, ["KTsh", "Qh"], [P(bs_)])
            if lt == 16:
                mm(bank(bs_)[0:nk_, 0:64], identb[0:64, 0:64], pens_t[0:64, :], False, True, ["identb", "pens"], [P(bs_)])
        ss_mm(0)
        for lt in range(17):
            nk = 128 if lt < 16 else 64
            bs = lt % 2
            pt = PTs[lt % 2]
            kpt = ("PTs", lt % 2)
            if lt + 1 < 17:
                ss_mm(lt + 1)
            act(pt[0:nk, :], bank(bs)[0:nk, 0:64], AF.Exp, [P(bs), "ncks"], [kpt], bias=ncks[0:nk, lt, h:h + 1])
            vl = cvh[:, lt, :] if lt < 16 else Vsn[0:64, h * 64:(h + 1) * 64]
            mm(bank(2)[0:64, 0:64], vl, pt[0:nk, :], lt == 0, lt == 16, ["cvh", "Vsn", kpt], [P(2)])
            mm(bank(3)[0:64, 0:64], onesb[0:nk, 0:64], pt[0:nk, :], lt == 0, lt == 16, ["onesb", kpt], [P(3)])
        S.op("dve", lambda e: e.reciprocal(out=rs[:, :], in_=bank(3)[0:64, 0:64]), [P(3)], ["rs"])
        tt("dve", OTs[:, :], bank(2)[0:64, 0:64], rs[:, :], ALU.mult, [P(2), "rs"], ["OTs"])
        for oc in range(8):
            mm(bank(4)[:, oc * 64:(oc + 1) * 64], wo_t[:, oc * 128:(oc + 1) * 128], OTs[:, :], True, True, ["wo", "OTs"], [P(4)])
        tt("dve", xT[:, :, S0:TN], bank(4).rearrange("p (a b) -> p a b", a=8), xT[:, :, S0:TN], ALU.add,
           [P(4)] + XT, XT)

    if dbg_stage == 6:
        return final_out(True)
    S.barrier()
    norm_full(6)
    ffn(3)

    return final_out(False)


_NC_CACHE = {}


def kernel(x_prompt, x_sample, cache_pool, cache_k, cache_v, cache_logf,
           ln_ffn1, ln_mix, ln_ffn2, w_ffn_in, w_ffn_out, w_pool, pool_scale,
           ln_kv, w_kv, w_fgate, b_fgate, w_q, w_o, ln_final):
    f32 = np.float32
    A = lambda a: np.ascontiguousarray(np.asarray(a, dtype=f32))
    x_prompt, x_sample, cache_pool, cache_k, cache_v, cache_logf = map(A, (x_prompt, x_sample, cache_pool, cache_k, cache_v, cache_logf))
    w_ffn_in, w_ffn_out, w_pool, w_kv, w_fgate, w_q, w_o = map(A, (w_ffn_in, w_ffn_out, w_pool, w_kv, w_fgate, w_q, w_o))
    stage = _NC_CACHE.get("stage", 99)
    if ("nc", stage) not in _NC_CACHE:
        _NC_CACHE[("nc", stage)] = build_program(stage, small=_NC_CACHE.get("small", False), sub=_NC_CACHE.get("sub", 0))
    nc = _NC_CACHE[("nc", stage)]
    wi = w_ffn_in.reshape(4, 8, 128, 2, 22, 128)
    win = np.ascontiguousarray(wi.transpose(0, 4, 2, 3, 1, 5)).reshape(4, 22, 128, 2048)
    wo_ = w_ffn_out.reshape(4, 2, 11, 128, 8, 128)
    wout = np.ascontiguousarray(wo_.transpose(0, 1, 4, 3, 2, 5)).reshape(4, 2, 8, 128, 1408)
    wp = w_pool[0].reshape(4, 2, 128, 256)
    wpool = np.ascontiguousarray(wp.transpose(2, 0, 1, 3)).reshape(128, 2048)
    wkv = np.ascontiguousarray(w_kv.reshape(8, 128, 2048).transpose(1, 0, 2)).reshape(128, 8 * 2048)
    wfg = np.ascontiguousarray(w_fgate.reshape(8, 128, 16).transpose(1, 0, 2)).reshape(128, 128)
    wq = np.ascontiguousarray(w_q[0].reshape(8, 128, 16, 64).transpose(2, 1, 0, 3)).reshape(16, 128, 512)
    wo = np.ascontiguousarray(w_o[0].reshape(16, 64, 1024))
    gains = [ln_ffn1[0], ln_mix[0], ln_ffn2[0], ln_kv, ln_ffn1[1], ln_mix[1], ln_ffn2[1], ln_final, pool_scale[0]]
    gall = np.ascontiguousarray(np.stack([A(g).reshape(8, 128).T for g in gains], axis=1)).reshape(128, 72)
    bfg = np.ascontiguousarray(np.broadcast_to(A(b_fgate)[None, :], (128, 16)))
    ident = np.eye(128, dtype=f32)
    tri = np.triu(np.ones((128, 128), f32))
    pens = np.zeros((128, 64), f32)
    pp_, jj_ = np.meshgrid(np.arange(128), np.arange(64), indexing="ij")
    pens[pp_ > jj_] = -30000.0
    in_maps = []
    for c in range(8):
        b, r = c // 4, c % 4
        xc = np.zeros((TN, 1024), f32)
        for m in range(4):
            g0 = 512 * (4 * m + r)
            if g0 >= 16:
                xc[16 * m:16 * m + 16] = x_prompt[b, g0 - 16:g0]
            xc[64 + 512 * m:64 + 512 * (m + 1)] = x_prompt[b, g0:g0 + 512]
        xc[S0:TN] = x_sample[c]
        cpool = np.zeros((16, 1024), f32)
        cpool[1:16] = cache_pool[0, c]
        clf = np.ascontiguousarray(cache_logf[c].reshape(16, 128, 16).transpose(1, 0, 2)).reshape(128, 256)
        invc = np.zeros((128, 4, 16), f32)
        for g in range(4):
            w = 2 ** (g + 1)
            pos = 512 * r + np.arange(16)
            invc[:, g, :] = (1.0 / np.minimum(pos + 1, w)).astype(f32)[None, :]
        mrank = np.zeros((128, 4), f32)
        mrank[:, r] = 1.0
        pen = np.zeros((16, 128, 512), f32)
        kk = (128 * np.arange(16)[:, None, None] + np.arange(128)[None, :, None])
        qq = 512 * r + np.arange(512)[None, None, :]
        pen[np.broadcast_to(kk > qq, pen.shape)] = -30000.0
        in_maps.append(dict(
            xc=xc, cpool=cpool, ck_in=cache_k[c].reshape(2048, 1024), cv_in=cache_v[c].reshape(2048, 1024), clf=clf,
            win=win, wout=wout, wpool=wpool, wkv=wkv, wfg=wfg, wq=wq, wo=wo, gall=gall, bfg=bfg, ident=ident, tri=tri,
            invc=invc.reshape(128, 64), mrank=mrank, pen=pen, pens=pens))
    if _NC_CACHE.get("maps_only"):
        return in_maps
    if _NC_CACHE.get("small", False):
        for mp in in_maps:
            mp["win"] = np.ascontiguousarray(win[0:1, 0:1])
            mp["wout"] = np.ascontiguousarray(wout[0:1, 0:1, 0:1])
    res = run_bass_kernel_spmd(nc, in_maps, core_ids=list(range(8)))
    R = res.results
    y_prompt = np.zeros((2, 8192, 1024), f32)
    k_prompt = np.zeros((2, 8192, 1024), f32)
    v_prompt = np.zeros((2, 8192, 1024), f32)
    logf_prompt = np.zeros((2, 8192, 16), f32)
    for c in range(8):
        b, r = c // 4, c % 4
        for m in range(4):
            g0 = 512 * (4 * m + r)
            sl = slice(512 * m, 512 * (m + 1))
            y_prompt[b, g0:g0 + 512] = R[c]["y_p"][sl]
            k_prompt[b, g0:g0 + 512] = R[c]["k_p"][sl]
            v_prompt[b, g0:g0 + 512] = R[c]["v_p"][sl]
            logf_prompt[b, g0:g0 + 512] = R[c]["lf_p"][sl]
    y_sample = np.stack([R[c]["y_s"] for c in range(8)])
    pool_prompt = np.stack([R[3]["pool_p"][1:16], R[7]["pool_p"][1:16]])[None]
    pool_sample = np.stack([R[c]["pool_s"][1:16] for c in range(8)])[None]
    k_sample = np.stack([R[c]["k_s"] for c in range(8)]).reshape(8, 64, 16, 64)
    v_sample = np.stack([R[c]["v_s"] for c in range(8)]).reshape(8, 64, 16, 64)
    logf_sample = np.stack([R[c]["lf_s"] for c in range(8)])
    return (y_prompt, y_sample, pool_prompt.astype(f32), pool_sample.astype(f32),
            k_prompt.reshape(2, 8192, 16, 64), v_prompt.reshape(2, 8192, 16, 64), logf_prompt,
            k_sample, v_sample, logf_sample)
```

```python
import concourse.bass as bass
import concourse.mybir as mybir

ENGS = ("pe", "act", "dve", "pool", "sp")


class Sched:
    def __init__(self, nc, ring=8):
        self.nc = nc
        self.ops = {e: [] for e in ENGS}
        self.state_w = {}
        self.state_r = {}
        self.ring = ring
        self.dma_count = {"sp": 0, "pool": 0, "act": 0}
        self.dma_ev = {"sp": [], "pool": [], "act": []}
        self.nseq = {e: 0 for e in ENGS}
        self.needed = {e: set() for e in ENGS}

    def _collect(self, eng, reads, writes):
        deps = {}

        def add(d, own_ok):
            for k, v in d.items():
                if k == ("eng", eng) and not own_ok and eng == "pe":
                    continue
                if deps.get(k, 0) < v:
                    deps[k] = v
        for key in reads:
            add(self.state_w.get(key, {}), True)
        for key in writes:
            add(self.state_w.get(key, {}), False)
            add(self.state_r.get(key, {}), False)
        return deps

    def _commit(self, evkey, val, reads, writes):
        for key in reads:
            d = self.state_r.setdefault(key, {})
            if d.get(evkey, 0) < val:
                d[evkey] = val
        for key in writes:
            d = self.state_w.setdefault(key, {})
            if d.get(evkey, 0) < val:
                d[evkey] = val
            self.state_r[key] = {}

    def op(self, eng, fn, reads=(), writes=()):
        deps = self._collect(eng, reads, writes)
        seq = self.nseq[eng]
        self.nseq[eng] += 1
        for k, v in deps.items():
            if k[0] == "eng":
                self.needed[k[1]].add(v - 1)
        self.ops[eng].append(dict(fn=fn, deps=deps, seq=seq, dma=None))
        self._commit(("eng", eng), seq + 1, reads, writes)

    def dma(self, q, fn, reads=(), writes=()):
        deps = self._collect(q, reads, writes)
        i = self.dma_count[q]
        self.dma_count[q] += 1
        slot = i % self.ring
        val = 16 * (i // self.ring + 1)
        evkey = ("dma", q, slot)
        if i >= self.ring:
            if deps.get(evkey, 0) < val - 16:
                deps[evkey] = val - 16
        seq = self.nseq[q]
        self.nseq[q] += 1
        for k, v in deps.items():
            if k[0] == "eng":
                self.needed[k[1]].add(v - 1)
        self.ops[q].append(dict(fn=fn, deps=deps, seq=seq, dma=(slot, val)))
        self._commit(evkey, val, reads, writes)
        return evkey, val

    def cc(self, fn, reads=(), writes=()):
        q = "pool"
        deps = self._collect(q, reads, writes)
        i = self.dma_count.get("cc", 0)
        self.dma_count["cc"] = i + 1
        evkey = ("cc",)
        val = i + 1
        seq = self.nseq[q]
        self.nseq[q] += 1
        for k, v in deps.items():
            if k[0] == "eng":
                self.needed[k[1]].add(v - 1)
        self.ops[q].append(dict(fn=fn, deps=deps, seq=seq, dma=("cc", val)))
        self._commit(evkey, val, reads, writes)

    def barrier(self):
        deps = {}
        for q in ("sp", "pool", "act"):
            n = self.dma_count[q]
            for slot in range(min(n, self.ring)):
                cnt = (n - 1 - slot) // self.ring + 1
                deps[("dma", q, slot)] = 16 * cnt
        for e in ENGS:
            last = None
            for o in reversed(self.ops[e]):
                if o["dma"] is None and o["fn"] is not None:
                    last = o["seq"]
                    break
            if last is not None:
                deps[("eng", e)] = last + 1
                self.needed[e].add(last)
        for e in ENGS:
            d = {k: v for k, v in deps.items() if k != ("eng", e)}
            seq = self.nseq[e]
            self.nseq[e] += 1
            self.ops[e].append(dict(fn=None, deps=d, seq=seq, dma=None))

    def final_wait_all(self, eng="sp"):
        deps = {}
        for q in ("sp", "pool", "act"):
            n = self.dma_count[q]
            for slot in range(min(n, self.ring)):
                cnt = (n - 1 - slot) // self.ring + 1
                deps[("dma", q, slot)] = 16 * cnt
        for e in ENGS:
            if e != eng and self.nseq[e] > 0:
                last = None
                for o in reversed(self.ops[e]):
                    if o["dma"] is None and o["fn"] is not None:
                        last = o["seq"]
                        break
                if last is not None:
                    deps[("eng", e)] = last + 1
                    self.needed[e].add(last)
        if self.dma_count.get("cc", 0):
            deps[("cc",)] = self.dma_count["cc"]
        seq = self.nseq[eng]
        self.nseq[eng] += 1
        self.ops[eng].append(dict(fn=None, deps=deps, seq=seq, dma=None))

    def emit(self, block, stack):
        nc = self.nc
        sems = {}

        def sem(name):
            if name not in sems:
                sems[name] = stack.enter_context(nc.semaphore(name))
            return sems[name]
        cum = {}
        for e in ENGS:
            c = 0
            m = {}
            for o in self.ops[e]:
                if o["dma"] is None and o["seq"] in self.needed[e]:
                    c += 1
                    m[o["seq"]] = c
            cum[e] = m
        for e in ENGS:
            sem("s_" + e)
        for q in ("sp", "pool", "act"):
            for s in range(min(self.dma_count[q], self.ring)):
                sem("d_%s_%d" % (q, s))
        if self.dma_count.get("cc", 0):
            sem("s_cc")

        def run(e, engine):
            waited = {}
            for o in self.ops[e]:
                for k, v in o["deps"].items():
                    if k[0] == "eng":
                        s = sems["s_" + k[1]]
                        tv = cum[k[1]][v - 1]
                    elif k[0] == "dma":
                        s = sems["d_%s_%d" % (k[1], k[2])]
                        tv = v
                    else:
                        s = sems["s_cc"]
                        tv = v
                    if waited.get(k, 0) < tv:
                        engine.wait_ge(s, tv)
                        waited[k] = tv
                if o["fn"] is None:
                    continue
                ins = o["fn"](engine)
                if o["dma"] is not None:
                    if o["dma"][0] == "cc":
                        ins.then_inc(sems["s_cc"], 1)
                    else:
                        ins.then_inc(sems["d_%s_%d" % (e, o["dma"][0])], 16)
                elif o["seq"] in self.needed[e]:
                    ins.then_inc(sems["s_" + e], 1)

        if self.ops["pe"]:
            @block.tensor
            def _(eng):
                run("pe", eng)
        if self.ops["act"]:
            @block.scalar
            def _(eng):
                run("act", eng)
        if self.ops["dve"]:
            @block.vector
            def _(eng):
                run("dve", eng)
        if self.ops["pool"]:
            @block.gpsimd
            def _(eng):
                run("pool", eng)
        if self.ops["sp"]:
            @block.sync
            def _(eng):
                run("sp", eng)

import numpy as np
from contextlib import ExitStack
from concourse.bass_utils import run_bass_kernel_spmd
import ml_dtypes

F32 = mybir.dt.float32
BF16 = mybir.dt.bfloat16
AF = mybir.ActivationFunctionType
ALU = mybir.AluOpType

TN = 2192
B0 = 64
HIST0 = 2112
S0 = 2128
GB = [0, 439, 878, 1317, 1755, 2192]
NGRP = 5
EPS = 1e-6
SB_BASE = 16512
SB_END = 229376


def build_program(dbg_stage=99, small=False, sub=0):
    nc = bass.Bass("TRN2", target_bir_lowering=False)

    def din(name, shape, dt=F32):
        return nc.dram_tensor(name, shape, dt, kind="ExternalInput").ap()

    def dout(name, shape, dt=F32):
        return nc.dram_tensor(name, shape, dt, kind="ExternalOutput").ap()

    xc = din("xc", [TN, 1024])
    cpool = din("cpool", [16, 1024])
    ck_in = din("ck_in", [2048, 1024])
    cv_in = din("cv_in", [2048, 1024])
    clf = din("clf", [128, 256])
    win = din("win", [4, 22, 128, 2048] if not small else [1, 1, 128, 2048])
    wout = din("wout", [4, 2, 8, 128, 1408] if not small else [1, 1, 1, 128, 1408])
    wpool = din("wpool", [128, 2048])
    wkv = din("wkv", [128, 8 * 2048])
    wfg = din("wfg", [128, 128])
    wq = din("wq", [16, 128, 512])
    wo = din("wo", [16, 64, 1024])
    gall = din("gall", [128, 72])
    bfg = din("bfg", [128, 16])
    ident_in = din("ident", [128, 128])
    tri_in = din("tri", [128, 128])
    invc_in = din("invc", [128, 64])
    mrank_in = din("mrank", [128, 4])
    pen_in = din("pen", [16, 128, 512])
    pens_in = din("pens", [128, 64])

    y_p = dout("y_p", [2048, 1024])
    y_s = dout("y_s", [64, 1024])
    pool_p = dout("pool_p", [16, 1024])
    pool_s = dout("pool_s", [16, 1024])
    k_p = dout("k_p", [2048, 1024])
    v_p = dout("v_p", [2048, 1024])
    lf_p = dout("lf_p", [2048, 16])
    k_s = dout("k_s", [64, 1024])
    v_s = dout("v_s", [64, 1024])
    lf_s = dout("lf_s", [64, 16])

    KTx = [nc.dram_tensor("KTx%d" % i, [128, 4096], BF16) for i in range(4)]
    KTall = [nc.dram_tensor("KTall%d" % i, [512, 4096], BF16) for i in range(4)]
    Vx = [nc.dram_tensor("Vx%d" % i, [128, 2560], BF16) for i in range(8)]
    Vall = [nc.dram_tensor("Vall%d" % i, [512, 2560], BF16) for i in range(8)]
    KTxv = [t.ap().rearrange("p (a c) -> (p a) c", a=2) for t in KTx]
    Vxv = [t.ap().rearrange("p (a c) -> (p a) c", a=2).rearrange("(h p) (l c) -> p h l c", p=128, c=80) for t in Vx]
    cw_x = nc.dram_tensor("cw_x", [128, 256], F32)
    cw_all = nc.dram_tensor("cw_all", [512, 256], F32)
    cq_d = nc.dram_tensor("cq_d", [16, 2112], BF16)

    S = Sched(nc, ring=8)
    GBm = list(GB)

    cur = [SB_BASE]

    def alloc(name, shape, dt, at=None):
        nbytes = int(np.prod(shape[1:])) * (4 if dt == F32 else 2)
        nbytes = (nbytes + 31) // 32 * 32
        if at is None:
            off = cur[0]
            cur[0] += nbytes
        else:
            off = at
        assert off + nbytes <= SB_END, (name, off, nbytes)
        return nc.alloc_sbuf_tensor_at(name, shape, dt, offset=off), off + nbytes

    xT, _ = alloc("xT", [128, 8, TN], F32)
    gal, _ = alloc("gal", [128, 72], F32)
    ident, _ = alloc("ident", [128, 128], F32)
    identb, _ = alloc("identb", [128, 128], BF16)
    tri, _ = alloc("tri", [128, 128], F32)
    onesb, _ = alloc("onesb", [128, 128], BF16)
    onesf, _ = alloc("onesf", [128, 128], F32)
    bfg_t, _ = alloc("bfg_t", [128, 16], F32)
    invc, _ = alloc("invc", [128, 4, 16], F32)
    mrank, _ = alloc("mrank", [128, 4], F32)
    epsb, _ = alloc("epsb", [128, 1], F32)
    one1, _ = alloc("one1", [128, 1], F32)
    KTs, _ = alloc("KTs", [64, 16, 64], BF16)
    Vsn, _ = alloc("Vsn", [128, 1024], BF16)
    lfall, _ = alloc("lfall", [128, 17, 16], F32)
    cwt, _ = alloc("cwt", [128, 17, 16], F32)
    cq, _ = alloc("cq", [128, 17, 16], F32)
    ncks, _ = alloc("ncks", [128, 17, 16], F32)
    upT, _ = alloc("upT", [128, 8, 32], F32)
    xnT, _ = alloc("xnT", [128, 8, TN], BF16)
    RSTD_OFF = cur[0]
    rstd, _ = alloc("rstd", [128, TN], F32)
    OV = cur[0]
    o = OV
    actT, o = alloc("actT", [128, 11, TN], BF16, at=o)
    win_t = []
    for i in range(3):
        t, o = alloc("win_t%d" % i, [128, 2, 8, 128], BF16, at=o)
        win_t.append(t)
    wout_t = []
    for i in range(2):
        t, o = alloc("wout_t%d" % i, [128, 11, 128], BF16, at=o)
        wout_t.append(t)
    wst = []
    for i in range(3):
        t, o = alloc("wst%d" % i, [128, 1024], F32, at=o)
        wst.append(t)
    sq = []
    for i in range(2):
        t, o = alloc("sq%d" % i, [128, 512], BF16, at=o)
        sq.append(t)
    sil = []
    for i in range(2):
        t, o = alloc("sil%d" % i, [128, 512], F32, at=o)
        sil.append(t)
    FFN_END = o
    o = OV
    xin = []
    for i in range(2):
        t, o = alloc("xin%d" % i, [128, 1024], F32, at=o)
        xin.append(t)
    o = OV
    ut, o = alloc("ut", [128, TN], F32, at=o)
    s1, o = alloc("s1", [128, TN], F32, at=o)
    s2, o = alloc("s2", [128, TN], F32, at=o)
    wpool_t, o = alloc("wpool_t", [128, 4, 2, 256], BF16, at=o)
    hist_in, o = alloc("hist_in", [16, 1024], F32, at=o)
    histT, o = alloc("histT", [128, 8, 16], F32, at=o)
    tmp16, o = alloc("tmp16", [128, 16], F32, at=o)
    pout, o = alloc("pout", [32, 1024], F32, at=o)
    o = OV
    wkv_t, o = alloc("wkv_t", [128, 8, 2048], BF16, at=o)
    wfg_t, o = alloc("wfg_t", [128, 8, 16], BF16, at=o)
    stage = []
    for i in range(2):
        t, o = alloc("stage%d" % i, [128, 2048], F32, at=o)
        stage.append(t)
    vst = []
    for i in range(2):
        t, o = alloc("vst%d" % i, [128, 16, 80], BF16, at=o)
        vst.append(t)
    kst = []
    for i in range(4):
        t, o = alloc("kst%d" % i, [128, 512], BF16, at=o)
        kst.append(t)
    zt = []
    for i in range(5):
        t, o = alloc("zt%d" % i, [128, 16], F32, at=o)
        zt.append(t)
    clf_t, o = alloc("clf_t", [128, 16, 16], F32, at=o)
    cwp, o = alloc("cwp", [128, 16, 16], F32, at=o)
    sc1, o = alloc("sc1", [128, 16, 16], F32, at=o)
    sc2, o = alloc("sc2", [128, 16, 16], F32, at=o)
    tots, o = alloc("tots", [128, 16, 16], F32, at=o)
    o = OV
    KTh, o = alloc("KTh", [128, 8192], BF16, at=o)
    Vh, o = alloc("Vh", [128, 64, 80], BF16, at=o)
    Qh, o = alloc("Qh", [128, 2112], BF16, at=o)
    pen_t, o = alloc("pen_t", [128, 16, 512], BF16, at=o)
    pens_t, o = alloc("pens_t", [128, 64], BF16, at=o)
    PT = []
    for i in range(4):
        t, o = alloc("PT%d" % i, [128, 512], BF16, at=o)
        PT.append(t)
    nck, o = alloc("nck", [128, 64, 16], F32, at=o)
    cwg, o = alloc("cwg", [128, 64, 16], F32, at=o)
    totb, o = alloc("totb", [128, 64, 16], F32, at=o)
    scA, o = alloc("scA", [128, 64, 16], F32, at=o)
    rd, o = alloc("rd", [65, 512], F32, at=o)
    rbs, o = alloc("rbs", [64, 512], F32, at=o)
    OTn, o = alloc("OTn", [64, 512], BF16, at=o)
    wq_t, o = alloc("wq_t", [128, 8, 64], BF16, at=o)
    wo_t, o = alloc("wo_t", [64, 1024], BF16, at=o)
    ckraw, o = alloc("ckraw", [128, 16, 64], BF16, at=o)
    cvh, o = alloc("cvh", [128, 16, 64], BF16, at=o)
    KTsh, o = alloc("KTsh", [128, 2112], BF16, at=o)
    PTs = []
    for i in range(2):
        t, o = alloc("PTs%d" % i, [128, 64], BF16, at=o)
        PTs.append(t)
    rs, o = alloc("rs", [64, 64], F32, at=o)
    OTs, o = alloc("OTs", [64, 64], BF16, at=o)
    cqT, _ = alloc("cqT", [16, 2112], BF16, at=RSTD_OFF)
    o = OV
    ytmp = []
    for i in range(2):
        t, o = alloc("ytmp%d" % i, [128, 128], F32, at=o)
        ytmp.append(t)
    ostage = []
    for i in range(2):
        t, o = alloc("ostage%d" % i, [128, 1024], F32, at=o)
        ostage.append(t)

    ps = nc.alloc_psum_tensor("ps", [128, 8, 512], F32)

    def bank(i):
        return ps[:, i, :]

    def P(i):
        return ("ps", i)

    def mm(out, lhsT, rhs, start, stop, reads, writes):
        S.op("pe", lambda e: e.matmul(out, lhsT=lhsT, rhs=rhs, start=start, stop=stop), reads, writes)

    def tr(out, in_, idn, reads, writes):
        S.op("pe", lambda e: e.transpose(out=out, in_=in_, identity=idn), reads, writes)

    def act(out, in_, func, reads, writes, bias=None, scale=1.0):
        if bias is None:
            S.op("act", lambda e: e.activation(out=out, in_=in_, func=func, scale=scale), reads, writes)
        else:
            S.op("act", lambda e: e.activation(out=out, in_=in_, func=func, bias=bias, scale=scale), reads, writes)

    def cp(eng, out, in_, reads, writes):
        if eng == "act":
            S.op("act", lambda e: e.copy(out=out, in_=in_), reads, writes)
        else:
            S.op(eng, lambda e: e.tensor_copy(out=out, in_=in_), reads, writes)

    def stt(eng, out, in0, scalar, in1, op0, op1, reads, writes):
        S.op(eng, lambda e: e.scalar_tensor_tensor(out=out, in0=in0, scalar=scalar, in1=in1, op0=op0, op1=op1),
             reads, writes)

    def tt(eng, out, in0, in1, op, reads, writes):
        S.op(eng, lambda e: e.tensor_tensor(out=out, in0=in0, in1=in1, op=op), reads, writes)

    def ts(eng, out, in0, s1_, s2_, op0, op1, reads, writes):
        S.op(eng, lambda e: e.tensor_scalar(out=out, in0=in0, scalar1=s1_, scalar2=s2_, op0=op0, op1=op1),
             reads, writes)

    def tss(eng, out, in0, s1_, op0, reads, writes):
        S.op(eng, lambda e: e.tensor_single_scalar(out=out, in_=in0, scalar=s1_, op=op0), reads, writes)

    def memset(eng, ap, val, writes):
        S.op(eng, lambda e: e.memset(ap, val), (), writes)

    def dma(q, out, in_, reads, writes):
        S.dma(q, lambda e: e.dma_start(out=out, in_=in_), reads, writes)

    XT = [("xT", k) for k in range(8)]
    XN = [("xn", k) for k in range(8)]

    dma("sp", gal[:, :], gall[:, :], (), ["gal"])
    dma("sp", ident[:, :], ident_in[:, :], (), ["ident"])
    dma("sp", tri[:, :], tri_in[:, :], (), ["tri"])
    dma("sp", bfg_t[:, :], bfg[:, :], (), ["bfg"])
    dma("sp", invc[:, :, :], invc_in.rearrange("p (g c) -> p g c", g=4), (), ["invc"])
    dma("sp", mrank[:, :], mrank_in[:, :], (), ["mrank"])
    memset("dve", onesb[:, :], 1.0, ["onesb"])
    memset("dve", onesf[:, :], 1.0, ["onesf"])
    memset("dve", epsb[:, :], EPS, ["epsb"])
    memset("dve", one1[:, :], 1.0, ["one1"])
    memset("dve", lfall[:, :, :], 0.0, ["lfall"])
    cp("dve", identb[:, :], ident[:, :], ["ident"], ["identb"])

    for t in range(18):
        r0 = 128 * t
        nr = min(128, TN - r0)
        xb = xin[t % 2]
        kx = ("xin", t % 2)
        dma("sp", xb[0:nr, :], xc[r0:r0 + nr, :], (), [kx])
        for half in range(2):
            bk = 6 + half
            for i in range(4):
                c = 4 * half + i
                tr(bank(bk)[:, i * 128:i * 128 + nr], xb[0:nr, c * 128:(c + 1) * 128], ident[0:nr, 0:nr],
                   [kx, "ident"], [P(bk)])
            src = bank(bk).rearrange("p (a b) -> p a b", a=4)[:, :, 0:nr]
            cp("act" if half == 0 else "dve", xT[:, 4 * half:4 * half + 4, r0:r0 + nr], src, [P(bk)],
               [("xT", 4 * half + i) for i in range(4)])

    S.barrier()
    def norm_stats_g(g):
        c0, c1 = GBm[g], GBm[g + 1]
        n = c1 - c0
        for kc in range(8):
            sb_ = sq[kc % 2]
            act(sb_[:, 0:n], xT[:, kc, c0:c1], AF.Square, [("xT", kc)], [("sq", kc % 2)])
            mm(bank(6)[:, 0:n], onesb[:, :], sb_[:, 0:n], kc == 0, kc == 7, [("sq", kc % 2), "onesb"], [P(6)])
        act(rstd[:, c0:c1], bank(6)[:, 0:n], AF.Sqrt, [P(6), "epsb"], [("rstd", g)], bias=epsb[:, 0:1], scale=1.0 / 1024)
        S.op("dve", lambda e, a=rstd[:, c0:c1]: e.reciprocal(out=a, in_=a), [("rstd", g)], [("rstd", g), "rstd"])

    def norm_stats():
        for g in range(NGRP):
            norm_stats_g(g)

    def norm_apply_g(gi, g):
        c0, c1 = GBm[g], GBm[g + 1]
        for kc in range(8):
            stt("dve", xnT[:, kc, c0:c1], xT[:, kc, c0:c1], gal[:, gi * 8 + kc:gi * 8 + kc + 1], rstd[:, c0:c1],
                ALU.mult, ALU.mult, [("xT", kc), ("rstd", g), "gal"], [("xn", kc)])

    def norm_apply(gi):
        pass

    def norm_full(gi):
        for g in range(NGRP):
            norm_stats_g(g)
            norm_apply_g(gi, g)

    wcnt = {"in": 0, "out": 0, "st": 0}

    def wload(dst3, src2, nk, wkey):
        sl = wcnt["st"] % 3
        wcnt["st"] += 1
        stg = wst[sl]
        dma("sp", stg[:, 0:nk * 128], src2, (), [("wst", sl)])
        cp("pool", dst3, stg[:, 0:nk * 128].rearrange("p (k n) -> p k n", k=nk), [("wst", sl)], [wkey])


    def ffn(f):
        for half in range(2):
            for ci in range(11):
                ffc = half * 11 + ci
                slot = wcnt["in"] % 3
                wcnt["in"] += 1
                wt = win_t[slot]
                wload(wt[:, 0, :, :], win[f, ffc][:, 0:1024], 8, ("win", slot))
                wload(wt[:, 1, :, :], win[f, ffc][:, 1024:2048], 8, ("win", slot))
                for g in range(NGRP):
                    c0, c1 = GBm[g], GBm[g + 1]
                    n = c1 - c0
                    bg = g % 2
                    bu = 2 + g % 2
                    for kc in range(8):
                        mm(bank(bg)[:, 0:n], wt[:, 0, kc, :], xnT[:, kc, c0:c1], kc == 0, kc == 7,
                           [("win", slot), ("xn", kc)], [P(bg)])
                    for kc in range(8):
                        mm(bank(bu)[:, 0:n], wt[:, 1, kc, :], xnT[:, kc, c0:c1], kc == 0, kc == 7,
                           [("win", slot), ("xn", kc)], [P(bu)])
                    st = sil[g % 2]
                    act(st[:, 0:n], bank(bg)[:, 0:n], AF.Silu, [P(bg)], [("sil", g % 2)])
                    tt("dve", actT[:, ci, c0:c1], st[:, 0:n], bank(bu)[:, 0:n], ALU.mult,
                       [("sil", g % 2), P(bu)], [("act", ci)])
            for oc in range(8):
                slot = wcnt["out"] % 2
                wcnt["out"] += 1
                wt = wout_t[slot]
                wload(wt[:, 0:8, :], wout[f, half, oc][:, 0:1024], 8, ("wout", slot))
                wload(wt[:, 8:11, :], wout[f, half, oc][:, 1024:1408], 3, ("wout", slot))
                for g in range(NGRP):
                    c0, c1 = GBm[g], GBm[g + 1]
                    n = c1 - c0
                    by = 4 + g % 2
                    for k in range(11):
                        mm(bank(by)[:, 0:n], wt[:, k, :], actT[:, k, c0:c1], k == 0, k == 10,
                           [("wout", slot), ("act", k)], [P(by)])
                    stt("dve", xT[:, oc, c0:c1], bank(by)[:, 0:n], 0.5, xT[:, oc, c0:c1], ALU.mult, ALU.add,
                        [P(by), ("xT", oc)], [("xT", oc)])

    def final_out(raw):
        S.barrier()
        if not raw:
            norm_stats()
        for (c0, nt, wi) in windows_f:
            og = ostage[wi % 2]
            ko = ("ostage", wi % 2)
            for kc in range(8):
                yt = ytmp[kc % 2]
                ky = ("ytmp", kc % 2)
                if raw:
                    cp("dve", yt[:, 0:nt], xT[:, kc, c0:c0 + nt], [("xT", kc)], [ky])
                else:
                    stt("dve", yt[:, 0:nt], xT[:, kc, c0:c0 + nt], gal[:, 56 + kc:57 + kc], rstd[:, c0:c0 + nt], ALU.mult, ALU.mult,
                        [("xT", kc), "rstd", "gal"], [ky])
                bk = 6 + kc // 4
                tr(bank(bk)[0:nt, (kc % 4) * 128:(kc % 4 + 1) * 128], yt[:, 0:nt], ident[:, :], [ky, "ident"], [P(bk)])
            cp("act", og[0:nt, 0:512], bank(6)[0:nt, :], [P(6)], [ko])
            cp("act", og[0:nt, 512:1024], bank(7)[0:nt, :], [P(7)], [ko])
            if wi < 16:
                dma("sp", y_p[wi * 128:(wi + 1) * 128, :], og[:, :], [ko], ["y_p"])
            else:
                dma("sp", y_s[:, :], og[0:64, :], [ko], ["y_s"])
        S.final_wait_all("sp")
        with ExitStack() as st:
            block = st.enter_context(nc.Block())
            S.emit(block, st)
        return nc

    windows_f = [(64 + 128 * lt, 128, lt) for lt in range(16)] + [(S0, 64, 16)]
    if dbg_stage == 0:
        return final_out(True)
    if small:
        def ffn(f):
            pass
    norm_full(0)
    ffn(0)

    if dbg_stage == 1:
        return final_out(True)
    S.barrier()
    norm_stats()
    dma("pool", wpool_t[:, :, :, :], wpool.rearrange("p (g k n) -> p g k n", g=4, k=2), (), ["wpool"])
    dma("sp", hist_in[:, :], cpool[:, :], (), ["hist_in"])
    for kc in range(8):
        tr(bank(7)[:, kc * 16:(kc + 1) * 16], hist_in[0:16, kc * 128:(kc + 1) * 128], ident[0:16, 0:16],
           ["hist_in", "ident"], [P(7)])
    cp("act", histT[:, :, :], bank(7)[:, 0:128].rearrange("p (a b) -> p a b", a=8), [P(7)], ["histT"])

    def useg(t_):
        v = t_[:, 0:2112].rearrange("p (m c) -> p m c", c=528)
        return v[:, :, 0:16], v[:, :, 16:528]

    for kc in range(8):
        grp = kc // 2
        w = 2 ** (grp + 1)
        gsc = gal[:, 8 + kc:8 + kc + 1]
        uh, ub = useg(ut)
        rh = rstd[:, 0:64].rearrange("p (m c) -> p m c", c=16)
        rb = rstd[:, 64:2112].rearrange("p (m c) -> p m c", c=512)
        xh = xT[:, kc, 0:64].rearrange("p (m c) -> p m c", c=16)
        xbk = xT[:, kc, 64:2112].rearrange("p (m c) -> p m c", c=512)
        stt("dve", uh, xh, gsc, rh, ALU.mult, ALU.mult, [("xT", kc), "rstd", "gal"], ["ut"])
        stt("dve", ub, xbk, gsc, rb, ALU.mult, ALU.mult, [("xT", kc), "rstd", "gal"], ["ut"])
        stt("dve", ut[:, S0:TN], xT[:, kc, S0:TN], gsc, rstd[:, S0:TN], ALU.mult, ALU.mult,
            [("xT", kc), "rstd", "gal"], ["ut"])
        cp("pool", ut[:, HIST0:S0], histT[:, kc, :], ["histT"], ["ut"])
        cp("act", upT[:, kc, 0:16], ut[:, 2096:2112], ["ut"], ["upT"])
        cp("act", upT[:, kc, 16:32], ut[:, 2176:2192], ["ut"], ["upT"])
        prev, pk = ut, "ut"
        pp = [(s1, "s1"), (s2, "s2")]
        for j in range(grp + 1):
            sh = 2 ** j
            nt_, nk_ = pp[j % 2]
            cp("dve", nt_[:, 0:sh], prev[:, 0:sh], [pk], [nk_])
            tt("dve", nt_[:, sh:TN], prev[:, sh:TN], prev[:, 0:TN - sh], ALU.add, [pk], [nk_])
            prev, pk = nt_, nk_
        _, sbv = useg(prev)
        dxb = xnT[:, kc, 64:2112].rearrange("p (m c) -> p m c", c=512)
        stt("dve", dxb, sbv, 1.0 / w, ub, ALU.mult, ALU.subtract, [pk, "ut"], [("xn", kc)])
        stt("dve", xnT[:, kc, S0:TN], prev[:, S0:TN], 1.0 / w, ut[:, S0:TN], ALU.mult, ALU.subtract,
            [pk, "ut"], [("xn", kc)])
        tt("dve", tmp16[:, :], prev[:, 16:32], invc[:, grp, :], ALU.mult, [pk, "invc"], ["tmp16"])
        tt("dve", xnT[:, kc, 64:80], tmp16[:, :], ut[:, 16:32], ALU.subtract, ["tmp16", "ut", ("xn", kc)], [("xn", kc)])
    colgroups = [(64 + 512 * m, 64 + 512 * (m + 1)) for m in range(4)] + [(S0, TN)]
    ib = 0
    for (c0, c1) in colgroups:
        n = c1 - c0
        for grp in range(4):
            for oc2 in range(2):
                bk = ib % 2
                ib += 1
                for k2 in range(2):
                    mm(bank(bk)[:, 0:n], wpool_t[:, grp, k2, oc2 * 128:(oc2 + 1) * 128], xnT[:, 2 * grp + k2, c0:c1],
                       k2 == 0, k2 == 1, ["wpool", ("xn", 2 * grp + k2)], [P(bk)])
                oc = 2 * grp + oc2
                stt("dve", xT[:, oc, c0:c1], bank(bk)[:, 0:n], gal[:, 64 + oc:64 + oc + 1], xT[:, oc, c0:c1],
                    ALU.mult, ALU.add, [P(bk), ("xT", oc), "gal"], [("xT", oc)])
    for kc in range(8):
        bk = 6 + kc // 4
        tr(bank(bk)[0:32, (kc % 4) * 128:(kc % 4 + 1) * 128], upT[:, kc, :], ident[:, :], ["upT", "ident"], [P(bk)])
    cp("act", pout[:, 0:512], bank(6)[0:32, :], [P(6)], ["pout"])
    cp("act", pout[:, 512:1024], bank(7)[0:32, :], [P(7)], ["pout"])
    dma("sp", pool_p[:, :], pout[0:16, :], ["pout"], ["pool_p"])
    dma("sp", pool_s[:, :], pout[16:32, :], ["pout"], ["pool_s"])

    if dbg_stage == 2:
        return final_out(True)
    GBm[:] = [64, 490, 916, 1342, 1768, 2192]
    S.barrier()
    norm_full(2)
    ffn(1)

    if dbg_stage == 3:
        return final_out(True)
    S.barrier()
    norm_full(3)
    for hf in range(2):
        for kc_ in range(8):
            wload(wkv_t[:, kc_, hf * 1024:(hf + 1) * 1024].rearrange("p (k n) -> p k n", k=8),
                  wkv[:, kc_ * 2048 + hf * 1024:kc_ * 2048 + (hf + 1) * 1024], 8, "wkv")
    dma("pool", wfg_t[:, :, :], wfg.rearrange("p (k n) -> p k n", k=8), (), ["wfg"])
    for i in range(2):
        if sub & 4:
            continue
        memset("pool", vst[i][:, :, :], 0.0, [("vst", i)])
        memset("pool", vst[i][:, :, 64:65], 1.0, [("vst", i)])
    windows = [(64 + 128 * lt, 128, lt) for lt in range(16)] + [(S0, 64, 16)]
    for (c0, nt, wi) in windows:
        if sub & 8:
            continue
        if (sub & 64) and wi >= 2:
            continue
        sg = stage[wi % 2]
        ksg = ("stage", wi % 2)
        vt = vst[wi % 2]
        kvt = ("vst", wi % 2)
        for cb in range(4):
            bk = cb % 2
            for kc in range(8):
                mm(bank(bk)[0:nt, :], xnT[:, kc, c0:c0 + nt], wkv_t[:, kc, cb * 512:(cb + 1) * 512], kc == 0, kc == 7,
                   [("xn", kc), "wkv"], [P(bk)])
            cp("act", sg[0:nt, cb * 512:(cb + 1) * 512], bank(bk)[0:nt, :], [P(bk)], [ksg])
            if cb >= 2 and not (sub & 16):
                srcv = sg[0:nt, cb * 512:(cb + 1) * 512].rearrange("p (h d) -> p h d", d=64)
                if wi < 16:
                    cp("dve", vt[0:nt, (cb - 2) * 8:(cb - 1) * 8, 0:64], srcv, [ksg], [kvt])
                else:
                    cp("dve", Vsn[0:nt, (cb - 2) * 512:(cb - 1) * 512], sg[0:nt, cb * 512:(cb + 1) * 512], [ksg], ["Vsn"])
        if wi < 16:
            if not (sub & 32):
                dma("sp", k_p[wi * 128:(wi + 1) * 128, :], sg[:, 0:1024], [ksg], ["k_p"])
                dma("sp", v_p[wi * 128:(wi + 1) * 128, :], sg[:, 1024:2048], [ksg], ["v_p"])
            for j in range(8):
                if sub & 2:
                    continue
                dma("sp", Vxv[j][:, :, wi, :], vt[:, 2 * j:2 * j + 2, :], [kvt], [("V_x", j)])
        else:
            dma("sp", k_s[:, :], sg[0:64, 0:1024], [ksg], ["k_s"])
            dma("sp", v_s[:, :], sg[0:64, 1024:2048], [ksg], ["v_s"])
        if sub & 1:
            continue
        for kc in range(8):
            mm(bank(2)[0:nt, 0:16], xnT[:, kc, c0:c0 + nt], wfg_t[:, kc, :], kc == 0, kc == 7, [("xn", kc), "wfg"], [P(2)])
        z, nz, e_, l_, m_ = [zt[i][0:nt, :] for i in range(5)]
        tt("dve", z, bank(2)[0:nt, 0:16], bfg_t[0:nt, :], ALU.add, [P(2), "bfg"], ["z0"])
        S.op("dve", lambda e, a=nz, b=z: e.tensor_scalar_mul(out=a, in0=b, scalar1=-1.0), ["z0"], ["z1"])
        tt("dve", nz, z, nz, ALU.min, ["z0", "z1"], ["z1"])
        act(e_, nz, AF.Exp, ["z1"], ["z2"])
        act(l_, e_, AF.Ln, ["z2", "one1"], ["z3"], bias=one1[0:nt, 0:1])
        S.op("dve", lambda e, a=m_, b=z: e.tensor_scalar_min(out=a, in0=b, scalar1=0.0), ["z0"], ["z4"])
        tt("dve", lfall[0:nt, wi, :], m_, l_, ALU.subtract, ["z4", "z3"], ["lfall"])
    for wi_ in range(16):
        dma("sp", lf_p[wi_ * 128:(wi_ + 1) * 128, :], lfall[:, wi_, :], ["lfall"], ["lf_p"])
    dma("sp", lf_s[:, :], lfall[0:64, 16, :], ["lfall"], ["lf_s"])
    if dbg_stage == 31:
        return final_out(True)
    ik = 0
    for hp in range(8):
        for gi_, (c0, c1) in enumerate(colgroups[0:4]):
            n = c1 - c0
            bk = ik % 2
            kb = kst[ik % 4]
            kk = ("kst", ik % 4)
            ik += 1
            for kc in range(8):
                mm(bank(bk)[:, 0:n], wkv_t[:, kc, hp * 128:(hp + 1) * 128], xnT[:, kc, c0:c1], kc == 0, kc == 7,
                   ["wkv", ("xn", kc)], [P(bk)])
            cp("act" if ik % 2 else "dve", kb[:, 0:n], bank(bk)[:, 0:n], [P(bk)], [kk])
            i4 = (2 * hp) // 4
            r0_ = ((2 * hp) % 4) * 64
            dma("sp", KTxv[i4][r0_:r0_ + 128, gi_ * 512:(gi_ + 1) * 512], kb[:, 0:n], [kk], [("KT_x", i4)])
    (c0, c1) = colgroups[4]
    for h in range(16):
        bk = 2 + h % 2
        for kc in range(8):
            mm(bank(bk)[0:64, 0:64], wkv_t[:, kc, h * 64:(h + 1) * 64], xnT[:, kc, c0:c1], kc == 0, kc == 7,
               ["wkv", ("xn", kc)], [P(bk)])
        cp("dve", KTs[:, h, :], bank(bk)[0:64, 0:64], [P(bk)], ["KTs"])
    if dbg_stage == 32:
        return final_out(True)
    mm(bank(3)[:, 0:272], tri[:, :], lfall[:, :, :].rearrange("p a b -> p (a b)"), True, True, ["tri", "lfall"], [P(3)])
    cp("dve", cwt[:, :, :].rearrange("p a b -> p (a b)"), bank(3)[:, 0:272], [P(3)], ["cwt"])
    dma("sp", cw_x.ap()[:, :], cwt[:, 0:16, :].rearrange("p a b -> p (a b)"), ["cwt"], ["cw_x"])
    if dbg_stage == 33:
        return final_out(True)
    RG = [[0, 1, 2, 3], [4, 5, 6, 7]]
    S.cc(lambda e: e.collective_compute("AllGather", ALU.bypass, replica_groups=RG, ins=[cw_x.ap().opt()],
                                        outs=[cw_all.ap().opt()]), ["cw_x"], ["cw_all"])
    for i in range(4):
        S.cc(lambda e, a=KTx[i], c=KTall[i]: e.collective_compute("AllGather", ALU.bypass, replica_groups=RG, ins=[a.ap().opt()],
                                                                  outs=[c.ap().opt()]), [("KT_x", i)], [("KT_all", i)])
    for j in range(8):
        S.cc(lambda e, a=Vx[j], c=Vall[j]: e.collective_compute("AllGather", ALU.bypass, replica_groups=RG, ins=[a.ap().opt()],
                                                                outs=[c.ap().opt()]), [("V_x", j)], [("V_all", j)])
    if dbg_stage == 34:
        return final_out(True)
    dma("sp", clf_t[:, :, :], clf.rearrange("p (l h) -> p l h", l=16), (), ["clf"])
    clf2 = clf_t[:, :, :].rearrange("p a b -> p (a b)")
    mm(bank(4)[:, 0:256], tri[:, :], clf2, True, True, ["tri", "clf"], [P(4)])
    mm(bank(5)[:, 0:256], onesf[:, :], clf2, True, True, ["onesf", "clf"], [P(5)])
    cp("dve", cwp[:, :, :].rearrange("p a b -> p (a b)"), bank(4)[:, 0:256], [P(4)], ["cwp"])
    cp("dve", tots[:, :, :].rearrange("p a b -> p (a b)"), bank(5)[:, 0:256], [P(5)], ["tots"])
    prev, pk = tots, "tots"
    pp = [(sc1, "sc1"), (sc2, "sc2")]
    for j in range(4):
        sh = 2 ** j
        nt_, nk_ = pp[j % 2]
        cp("dve", nt_[:, 0:sh, :], prev[:, 0:sh, :], [pk], [nk_])
        tt("dve", nt_[:, sh:16, :], prev[:, sh:16, :], prev[:, 0:16 - sh, :], ALU.add, [pk], [nk_])
        prev, pk = nt_, nk_
    incl, ik_ = prev, pk
    tt("dve", cq[:, 0:16, :], incl[:, :, :], tots[:, :, :], ALU.subtract, [ik_, "tots"], ["cqtmp"])
    tt("dve", cq[:, 0:16, :], cq[:, 0:16, :], cwp[:, :, :], ALU.add, ["cqtmp", "cwp"], ["cqtmp"])
    S.op("dve", lambda e: e.tensor_scalar_mul(out=ncks[:, 0:16, :], in0=cq[:, 0:16, :], scalar1=-1.0), ["cqtmp"], ["ncks"])
    tt("dve", cq[:, 16, :], cwt[:, 16, :], incl[:, 15, :], ALU.add, ["cwt", ik_, "cqtmp"], ["cq"])
    S.op("dve", lambda e: e.tensor_scalar_mul(out=ncks[:, 16, :], in0=cq[:, 16, :], scalar1=-1.0), ["cq"], ["ncks"])

    if dbg_stage == 4:
        return final_out(True)
    S.barrier()
    norm_full(4)
    ffn(2)
    norm_full(5)
    S.barrier()

    if dbg_stage == 5:
        return final_out(True)
    cwav = cw_all.ap().rearrange("(r p) (m c) -> r p m c", p=128, m=4)
    cwgv = cwg[:, :, :].rearrange("p (m r w) h -> p m r (w h)", m=4, r=4)
    totv = totb[:, :, :].rearrange("p (m r w) h -> p m r (w h)", m=4, r=4)
    for r in range(4):
        dma("sp", cwgv[:, :, r, :], cwav[r], ["cw_all"], ["cwg"])
        for m in range(4):
            dma("sp", totv[:, m, r, :], cw_all.ap()[r * 128 + 127:r * 128 + 128, m * 64:(m + 1) * 64].to_broadcast([128, 64]),
                ["cw_all"], ["totb"])
    prev, pk = totb, "totb"
    pp = [(scA, "scA"), (nck, "nck")]
    for j in range(6):
        sh = 2 ** j
        nt_, nk_ = pp[j % 2]
        cp("dve", nt_[:, 0:sh, :], prev[:, 0:sh, :], [pk], [nk_])
        tt("dve", nt_[:, sh:64, :], prev[:, sh:64, :], prev[:, 0:64 - sh, :], ALU.add, [pk], [nk_])
        prev, pk = nt_, nk_
    tt("dve", scA[:, :, :], nck[:, :, :], totb[:, :, :], ALU.subtract, ["nck", "totb"], ["scA"])
    tt("dve", scA[:, :, :], scA[:, :, :], cwg[:, :, :], ALU.add, ["scA", "cwg"], ["scA"])
    S.op("dve", lambda e: e.tensor_scalar_mul(out=nck[:, :, :], in0=scA[:, :, :], scalar1=-1.0), ["scA"], ["nck"])
    ckv = scA[:, :, :].rearrange("p (m r w) h -> p m r (w h)", m=4, r=4)
    cqv = cq[:, 0:16, :].rearrange("p (m w) h -> p m (w h)", m=4)
    S.op("dve", lambda e: e.tensor_scalar_mul(out=cqv, in0=ckv[:, :, 0, :], scalar1=mrank[:, 0:1]), ["scA", "mrank", "cq"], ["cq"])
    for r in range(1, 4):
        stt("dve", cqv, ckv[:, :, r, :], mrank[:, r:r + 1], cqv, ALU.mult, ALU.add, ["scA", "mrank", "cq"], ["cq"])
    for rnd in range(5):
        nl = 4 if rnd < 4 else 1
        for i in range(nl):
            lt = rnd * 4 + i
            nt = 128 if lt < 16 else 64
            tr(bank(7)[0:16, i * 128:i * 128 + nt], cq[0:nt, lt, :], ident[0:nt, 0:nt], ["cq", "ident"], [P(7)])
        ncols = 512 if rnd < 4 else 64
        cp("act", cqT[:, rnd * 512:rnd * 512 + ncols], bank(7)[0:16, 0:ncols], [P(7)], ["cqT"])
    dma("sp", cq_d.ap()[:, :], cqT[:, :], ["cqT"], ["cq_d"])

    dma("pool", pen_t[:, :, :], pen_in.rearrange("k p j -> p k j"), (), ["pen"])
    dma("pool", pens_t[:, :], pens_in[:, :], (), ["pens"])
    memset("dve", KTh[64:128, :], 0.0, ["KTh"])
    memset("dve", KTsh[64:128, :], 0.0, ["KTsh"])
    memset("dve", Qh[64:128, :], 0.0, ["Qh"])
    memset("dve", KTh[64:65, :], 1.0, ["KTh"])
    memset("dve", KTsh[64:65, :], 1.0, ["KTsh"])
    KTav = [t.ap().rearrange("p (a c) -> (p a) c", a=2).rearrange("(r q) (m c) -> r q m c", r=4, m=4) for t in KTall]
    KThv = KTh[0:64, :].rearrange("p (m r c) -> p m r c", m=4, r=4)
    Vav = [t.ap().rearrange("p (a c) -> (p a) c", a=2).rearrange("(r q) (m c) -> r q m c", r=4, m=4) for t in Vall]
    Vhv = Vh[:, :, :].rearrange("p (m r w) c -> p m r (w c)", m=4, r=4)
    ckv_in = ck_in.rearrange("(l p) (h d) -> p l h d", p=128, d=64)
    cvv_in = cv_in.rearrange("(l p) (h d) -> p l h d", p=128, d=64)
    for h in range(16):
        dma("pool", wq_t[:, :, :], wq[h].rearrange("p (k n) -> p k n", k=8), (), ["wq"])
        dma("pool", wo_t[:, :], wo[h], (), ["wo"])
        for r in range(4):
            dma("sp", KThv[:, :, r, :], KTav[h // 4][r, (h % 4) * 64:(h % 4 + 1) * 64], [("KT_all", h // 4)], ["KTh"])
            dma("sp", Vhv[:, :, r, :], Vav[h // 2][r, (h % 2) * 128:(h % 2 + 1) * 128], [("V_all", h // 2)], ["Vh"])
        dma("pool", ckraw[:, :, :], ckv_in[:, :, h, :], (), ["ckraw"])
        dma("pool", cvh[:, :, :], cvv_in[:, :, h, :], (), ["cvh"])
        for gi_, (c0, c1) in enumerate(colgroups):
            n = c1 - c0
            for kc in range(8):
                mm(bank(7)[0:64, 0:n], wq_t[:, kc, :], xnT[:, kc, c0:c1], kc == 0, kc == 7, ["wq", ("xn", kc)], [P(7)])
            S.op("act", lambda e, a=Qh[0:64, gi_ * 512:gi_ * 512 + n], b=bank(7)[0:64, 0:n]: e.mul(out=a, in_=b, mul=0.125),
                 [P(7)], ["Qh"])
        dma("sp", Qh[64:65, :], cq_d.ap()[h:h + 1, :], ["cq_d"], ["Qh"])
        it = 0
        pending = []
        for m in range(4):
            nkt = 16 * (m + 1)
            qcols = Qh[:, m * 512:(m + 1) * 512]
            bo = 2 + m % 2
            SB_ = (0, 1, 5, 7)

            def s_mm(kt):
                bs_ = SB_[kt % 4]
                diag_ = kt >= 16 * m
                mm(bank(bs_), KTh[:, kt * 128:(kt + 1) * 128], qcols, True, not diag_, ["KTh", "Qh"], [P(bs_)])
                if diag_:
                    mm(bank(bs_), identb[:, :], pen_t[:, kt - 16 * m, :], False, True, ["identb", "pen"], [P(bs_)])
            for k0 in range(4):
                s_mm(k0)
            for j in range(nkt // 2):
                ka, kb = 2 * j, 2 * j + 1
                act(PT[ka % 4][:, :], bank(SB_[ka % 4]), AF.Exp, [P(SB_[ka % 4]), P(SB_[kb % 4]), "nck"], [("PT", ka % 4)],
                    bias=nck[:, ka, h:h + 1])
                act(PT[kb % 4][:, :], bank(SB_[kb % 4]), AF.Exp, [P(SB_[kb % 4]), "nck"], [("PT", kb % 4)],
                    bias=nck[:, kb, h:h + 1])
                mm(bank(bo)[0:65, :], Vh[:, ka, 0:65], PT[ka % 4][:, :], ka == 0, False,
                   ["Vh", ("PT", ka % 4), ("PT", kb % 4)], [P(bo)])
                mm(bank(bo)[0:65, :], Vh[:, kb, 0:65], PT[kb % 4][:, :], False, kb == nkt - 1,
                   ["Vh", ("PT", kb % 4)], [P(bo)])
                if ka + 4 < nkt:
                    s_mm(ka + 4)
                    s_mm(kb + 4)
                if j >= 2 and pending:
                    pending.pop(0)()

            def make_epi(m_=m, bo_=bo):
                def st_a():
                    cp("act", rd[64:65, :], bank(bo_)[64:65, :], [P(bo_)], ["rd"])
                    S.op("dve", lambda e: e.reciprocal(out=rd[64:65, :], in_=rd[64:65, :]), ["rd"], ["rd"])

                def st_b():
                    mm(bank(6)[0:64, :], onesf[64:65, 0:64], rd[64:65, :], True, True, ["onesf", "rd"], [P(6)])
                    cp("act", rbs[:, :], bank(6)[0:64, :], [P(6)], ["rbs"])
                    tt("dve", OTn[:, :], bank(bo_)[0:64, :], rbs[:, :], ALU.mult, [P(bo_), "rbs"], ["OTn"])

                def st_c(oc):
                    def f():
                        by = (4, 6)[oc % 2]
                        mm(bank(by), wo_t[:, oc * 128:(oc + 1) * 128], OTn[:, :], True, True, ["wo", "OTn"], [P(by)])
                        c0_ = 64 + 512 * m_
                        tt("dve", xT[:, oc, c0_:c0_ + 512], bank(by), xT[:, oc, c0_:c0_ + 512], ALU.add,
                           [P(by), ("xT", oc)], [("xT", oc)])
                    return f
                return [st_a, (lambda: None), (lambda: None), (lambda: None), st_b] + [st_c(oc) for oc in range(8)]
            assert not pending
            pending.extend(make_epi())
        while pending:
            pending.pop(0)()
        bb = bank(7).bitcast(BF16)
        for rnd in range(2):
            for i in range(8):
                lt = rnd * 8 + i
                tr(bb[0:64, i * 128:(i + 1) * 128], ckraw[:, lt, :], identb[:, :], ["ckraw", "identb"], [P(7)])
            cp("dve", KTsh[0:64, rnd * 1024:(rnd + 1) * 1024], bb[0:64, :], [P(7)], ["KTsh"])
        cp("dve", KTsh[0:64, 2048:2112], KTs[:, h, :], ["KTs"], ["KTsh"])
        qs = Qh[:, 2048:2112]
        def ss_mm(lt):
            nk_ = 128 if lt < 16 else 64
            bs_ = lt % 2
            mm(bank(bs_)[0:nk_, 0:64], KTsh[:, lt * 128:lt * 128 + nk_], qs, True, lt < 16, ["KTsh", "Qh"], [P(bs_)])
            if lt == 16:
                mm(bank(bs_)[0:nk_, 0:64], identb[0:64, 0:64], pens_t[0:64, :], False, True, ["identb", "pens"], [P(bs_)])
        ss_mm(0)
        for lt in range(17):
            nk = 128 if lt < 16 else 64
            bs = lt % 2
            pt = PTs[lt % 2]
            kpt = ("PTs", lt % 2)
            if lt + 1 < 17:
                ss_mm(lt + 1)
            act(pt[0:nk, :], bank(bs)[0:nk, 0:64], AF.Exp, [P(bs), "ncks"], [kpt], bias=ncks[0:nk, lt, h:h + 1])
            vl = cvh[:, lt, :] if lt < 16 else Vsn[0:64, h * 64:(h + 1) * 64]
            mm(bank(2)[0:64, 0:64], vl, pt[0:nk, :], lt == 0, lt == 16, ["cvh", "Vsn", kpt], [P(2)])
            mm(bank(3)[0:64, 0:64], onesb[0:nk, 0:64], pt[0:nk, :], lt == 0, lt == 16, ["onesb", kpt], [P(3)])
        S.op("dve", lambda e: e.reciprocal(out=rs[:, :], in_=bank(3)[0:64, 0:64]), [P(3)], ["rs"])
        tt("dve", OTs[:, :], bank(2)[0:64, 0:64], rs[:, :], ALU.mult, [P(2), "rs"], ["OTs"])
        for oc in range(8):
            mm(bank(4)[:, oc * 64:(oc + 1) * 64], wo_t[:, oc * 128:(oc + 1) * 128], OTs[:, :], True, True, ["wo", "OTs"], [P(4)])
        tt("dve", xT[:, :, S0:TN], bank(4).rearrange("p (a b) -> p a b", a=8), xT[:, :, S0:TN], ALU.add,
           [P(4)] + XT, XT)

    if dbg_stage == 6:
        return final_out(True)
    S.barrier()
    norm_full(6)
    ffn(3)

    return final_out(False)


_NC_CACHE = {}


def kernel(x_prompt, x_sample, cache_pool, cache_k, cache_v, cache_logf,
           ln_ffn1, ln_mix, ln_ffn2, w_ffn_in, w_ffn_out, w_pool, pool_scale,
           ln_kv, w_kv, w_fgate, b_fgate, w_q, w_o, ln_final):
    f32 = np.float32
    A = lambda a: np.ascontiguousarray(np.asarray(a, dtype=f32))
    x_prompt, x_sample, cache_pool, cache_k, cache_v, cache_logf = map(A, (x_prompt, x_sample, cache_pool, cache_k, cache_v, cache_logf))
    w_ffn_in, w_ffn_out, w_pool, w_kv, w_fgate, w_q, w_o = map(A, (w_ffn_in, w_ffn_out, w_pool, w_kv, w_fgate, w_q, w_o))
    stage = _NC_CACHE.get("stage", 99)
    if ("nc", stage) not in _NC_CACHE:
        _NC_CACHE[("nc", stage)] = build_program(stage, small=_NC_CACHE.get("small", False), sub=_NC_CACHE.get("sub", 0))
    nc = _NC_CACHE[("nc", stage)]
    wi = w_ffn_in.reshape(4, 8, 128, 2, 22, 128)
    win = np.ascontiguousarray(wi.transpose(0, 4, 2, 3, 1, 5)).reshape(4, 22, 128, 2048)
    wo_ = w_ffn_out.reshape(4, 2, 11, 128, 8, 128)
    wout = np.ascontiguousarray(wo_.transpose(0, 1, 4, 3, 2, 5)).reshape(4, 2, 8, 128, 1408)
    wp = w_pool[0].reshape(4, 2, 128, 256)
    wpool = np.ascontiguousarray(wp.transpose(2, 0, 1, 3)).reshape(128, 2048)
    wkv = np.ascontiguousarray(w_kv.reshape(8, 128, 2048).transpose(1, 0, 2)).reshape(128, 8 * 2048)
    wfg = np.ascontiguousarray(w_fgate.reshape(8, 128, 16).transpose(1, 0, 2)).reshape(128, 128)
    wq = np.ascontiguousarray(w_q[0].reshape(8, 128, 16, 64).transpose(2, 1, 0, 3)).reshape(16, 128, 512)
    wo = np.ascontiguousarray(w_o[0].reshape(16, 64, 1024))
    gains = [ln_ffn1[0], ln_mix[0], ln_ffn2[0], ln_kv, ln_ffn1[1], ln_mix[1], ln_ffn2[1], ln_final, pool_scale[0]]
    gall = np.ascontiguousarray(np.stack([A(g).reshape(8, 128).T for g in gains], axis=1)).reshape(128, 72)
    bfg = np.ascontiguousarray(np.broadcast_to(A(b_fgate)[None, :], (128, 16)))
    ident = np.eye(128, dtype=f32)
    tri = np.triu(np.ones((128, 128), f32))
    pens = np.zeros((128, 64), f32)
    pp_, jj_ = np.meshgrid(np.arange(128), np.arange(64), indexing="ij")
    pens[pp_ > jj_] = -30000.0
    in_maps = []
    for c in range(8):
        b, r = c // 4, c % 4
        xc = np.zeros((TN, 1024), f32)
        for m in range(4):
            g0 = 512 * (4 * m + r)
            if g0 >= 16:
                xc[16 * m:16 * m + 16] = x_prompt[b, g0 - 16:g0]
            xc[64 + 512 * m:64 + 512 * (m + 1)] = x_prompt[b, g0:g0 + 512]
        xc[S0:TN] = x_sample[c]
        cpool = np.zeros((16, 1024), f32)
        cpool[1:16] = cache_pool[0, c]
        clf = np.ascontiguousarray(cache_logf[c].reshape(16, 128, 16).transpose(1, 0, 2)).reshape(128, 256)
        invc = np.zeros((128, 4, 16), f32)
        for g in range(4):
            w = 2 ** (g + 1)
            pos = 512 * r + np.arange(16)
            invc[:, g, :] = (1.0 / np.minimum(pos + 1, w)).astype(f32)[None, :]
        mrank = np.zeros((128, 4), f32)
        mrank[:, r] = 1.0
        pen = np.zeros((16, 128, 512), f32)
        kk = (128 * np.arange(16)[:, None, None] + np.arange(128)[None, :, None])
        qq = 512 * r + np.arange(512)[None, None, :]
        pen[np.broadcast_to(kk > qq, pen.shape)] = -30000.0
        in_maps.append(dict(
            xc=xc, cpool=cpool, ck_in=cache_k[c].reshape(2048, 1024), cv_in=cache_v[c].reshape(2048, 1024), clf=clf,
            win=win, wout=wout, wpool=wpool, wkv=wkv, wfg=wfg, wq=wq, wo=wo, gall=gall, bfg=bfg, ident=ident, tri=tri,
            invc=invc.reshape(128, 64), mrank=mrank, pen=pen, pens=pens))
    if _NC_CACHE.get("maps_only"):
        return in_maps
    if _NC_CACHE.get("small", False):
        for mp in in_maps:
            mp["win"] = np.ascontiguousarray(win[0:1, 0:1])
            mp["wout"] = np.ascontiguousarray(wout[0:1, 0:1, 0:1])
    res = run_bass_kernel_spmd(nc, in_maps, core_ids=list(range(8)))
    R = res.results
    y_prompt = np.zeros((2, 8192, 1024), f32)
    k_prompt = np.zeros((2, 8192, 1024), f32)
    v_prompt = np.zeros((2, 8192, 1024), f32)
    logf_prompt = np.zeros((2, 8192, 16), f32)
    for c in range(8):
        b, r = c // 4, c % 4
        for m in range(4):
            g0 = 512 * (4 * m + r)
            sl = slice(512 * m, 512 * (m + 1))
            y_prompt[b, g0:g0 + 512] = R[c]["y_p"][sl]
            k_prompt[b, g0:g0 + 512] = R[c]["k_p"][sl]
            v_prompt[b, g0:g0 + 512] = R[c]["v_p"][sl]
            logf_prompt[b, g0:g0 + 512] = R[c]["lf_p"][sl]
    y_sample = np.stack([R[c]["y_s"] for c in range(8)])
    pool_prompt = np.stack([R[3]["pool_p"][1:16], R[7]["pool_p"][1:16]])[None]
    pool_sample = np.stack([R[c]["pool_s"][1:16] for c in range(8)])[None]
    k_sample = np.stack([R[c]["k_s"] for c in range(8)]).reshape(8, 64, 16, 64)
    v_sample = np.stack([R[c]["v_s"] for c in range(8)]).reshape(8, 64, 16, 64)
    logf_sample = np.stack([R[c]["lf_s"] for c in range(8)])
    return (y_prompt, y_sample, pool_prompt.astype(f32), pool_sample.astype(f32),
            k_prompt.reshape(2, 8192, 16, 64), v_prompt.reshape(2, 8192, 16, 64), logf_prompt,
            k_sample, v_sample, logf_sample)
```

```python
import concourse.bass as bass
import concourse.mybir as mybir

ENGS = ("pe", "act", "dve", "pool", "sp")


class Sched:
    def __init__(self, nc, ring=8):
        self.nc = nc
        self.ops = {e: [] for e in ENGS}
        self.state_w = {}
        self.state_r = {}
        self.ring = ring
        self.dma_count = {"sp": 0, "pool": 0, "act": 0}
        self.dma_ev = {"sp": [], "pool": [], "act": []}
        self.nseq = {e: 0 for e in ENGS}
        self.needed = {e: set() for e in ENGS}

    def _collect(self, eng, reads, writes):
        deps = {}

        def add(d, own_ok):
            for k, v in d.items():
                if k == ("eng", eng) and not own_ok and eng == "pe":
                    continue
                if deps.get(k, 0) < v:
                    deps[k] = v
        for key in reads:
            add(self.state_w.get(key, {}), True)
        for key in writes:
            add(self.state_w.get(key, {}), False)
            add(self.state_r.get(key, {}), False)
        return deps

    def _commit(self, evkey, val, reads, writes):
        for key in reads:
            d = self.state_r.setdefault(key, {})
            if d.get(evkey, 0) < val:
                d[evkey] = val
        for key in writes:
            d = self.state_w.setdefault(key, {})
            if d.get(evkey, 0) < val:
                d[evkey] = val
            self.state_r[key] = {}

    def op(self, eng, fn, reads=(), writes=()):
        deps = self._collect(eng, reads, writes)
        seq = self.nseq[eng]
        self.nseq[eng] += 1
        for k, v in deps.items():
            if k[0] == "eng":
                self.needed[k[1]].add(v - 1)
        self.ops[eng].append(dict(fn=fn, deps=deps, seq=seq, dma=None))
        self._commit(("eng", eng), seq + 1, reads, writes)

    def dma(self, q, fn, reads=(), writes=()):
        deps = self._collect(q, reads, writes)
        i = self.dma_count[q]
        self.dma_count[q] += 1
        slot = i % self.ring
        val = 16 * (i // self.ring + 1)
        evkey = ("dma", q, slot)
        if i >= self.ring:
            if deps.get(evkey, 0) < val - 16:
                deps[evkey] = val - 16
        seq = self.nseq[q]
        self.nseq[q] += 1
        for k, v in deps.items():
            if k[0] == "eng":
                self.needed[k[1]].add(v - 1)
        self.ops[q].append(dict(fn=fn, deps=deps, seq=seq, dma=(slot, val)))
        self._commit(evkey, val, reads, writes)
        return evkey, val

    def cc(self, fn, reads=(), writes=()):
        q = "pool"
        deps = self._collect(q, reads, writes)
        i = self.dma_count.get("cc", 0)
        self.dma_count["cc"] = i + 1
        evkey = ("cc",)
        val = i + 1
        seq = self.nseq[q]
        self.nseq[q] += 1
        for k, v in deps.items():
            if k[0] == "eng":
                self.needed[k[1]].add(v - 1)
        self.ops[q].append(dict(fn=fn, deps=deps, seq=seq, dma=("cc", val)))
        self._commit(evkey, val, reads, writes)

    def barrier(self):
        deps = {}
        for q in ("sp", "pool", "act"):
            n = self.dma_count[q]
            for slot in range(min(n, self.ring)):
                cnt = (n - 1 - slot) // self.ring + 1
                deps[("dma", q, slot)] = 16 * cnt
        for e in ENGS:
            last = None
            for o in reversed(self.ops[e]):
                if o["dma"] is None and o["fn"] is not None:
                    last = o["seq"]
                    break
            if last is not None:
                deps[("eng", e)] = last + 1
                self.needed[e].add(last)
        for e in ENGS:
            d = {k: v for k, v in deps.items() if k != ("eng", e)}
            seq = self.nseq[e]
            self.nseq[e] += 1
            self.ops[e].append(dict(fn=None, deps=d, seq=seq, dma=None))

    def final_wait_all(self, eng="sp"):
        deps = {}
        for q in ("sp", "pool", "act"):
            n = self.dma_count[q]
            for slot in range(min(n, self.ring)):
                cnt = (n - 1 - slot) // self.ring + 1
                deps[("dma", q, slot)] = 16 * cnt
        for e in ENGS:
            if e != eng and self.nseq[e] > 0:
                last = None
                for o in reversed(self.ops[e]):
                    if o["dma"] is None and o["fn"] is not None:
                        last = o["seq"]
                        break
                if last is not None:
                    deps[("eng", e)] = last + 1
                    self.needed[e].add(last)
        if self.dma_count.get("cc", 0):
            deps[("cc",)] = self.dma_count["cc"]
        seq = self.nseq[eng]
        self.nseq[eng] += 1
        self.ops[eng].append(dict(fn=None, deps=deps, seq=seq, dma=None))

    def emit(self, block, stack):
        nc = self.nc
        sems = {}

        def sem(name):
            if name not in sems:
                sems[name] = stack.enter_context(nc.semaphore(name))
            return sems[name]
        cum = {}
        for e in ENGS:
            c = 0
            m = {}
            for o in self.ops[e]:
                if o["dma"] is None and o["seq"] in self.needed[e]:
                    c += 1
                    m[o["seq"]] = c
            cum[e] = m
        for e in ENGS:
            sem("s_" + e)
        for q in ("sp", "pool", "act"):
            for s in range(min(self.dma_count[q], self.ring)):
                sem("d_%s_%d" % (q, s))
        if self.dma_count.get("cc", 0):
            sem("s_cc")

        def run(e, engine):
            waited = {}
            for o in self.ops[e]:
                for k, v in o["deps"].items():
                    if k[0] == "eng":
                        s = sems["s_" + k[1]]
                        tv = cum[k[1]][v - 1]
                    elif k[0] == "dma":
                        s = sems["d_%s_%d" % (k[1], k[2])]
                        tv = v
                    else:
                        s = sems["s_cc"]
                        tv = v
                    if waited.get(k, 0) < tv:
                        engine.wait_ge(s, tv)
                        waited[k] = tv
                if o["fn"] is None:
                    continue
                ins = o["fn"](engine)
                if o["dma"] is not None:
                    if o["dma"][0] == "cc":
                        ins.then_inc(sems["s_cc"], 1)
                    else:
                        ins.then_inc(sems["d_%s_%d" % (e, o["dma"][0])], 16)
                elif o["seq"] in self.needed[e]:
                    ins.then_inc(sems["s_" + e], 1)

        if self.ops["pe"]:
            @block.tensor
            def _(eng):
                run("pe", eng)
        if self.ops["act"]:
            @block.scalar
            def _(eng):
                run("act", eng)
        if self.ops["dve"]:
            @block.vector
            def _(eng):
                run("dve", eng)
        if self.ops["pool"]:
            @block.gpsimd
            def _(eng):
                run("pool", eng)
        if self.ops["sp"]:
            @block.sync
            def _(eng):
                run("sp", eng)

import numpy as np
from contextlib import ExitStack
from concourse.bass_utils import run_bass_kernel_spmd
import ml_dtypes

F32 = mybir.dt.float32
BF16 = mybir.dt.bfloat16
AF = mybir.ActivationFunctionType
ALU = mybir.AluOpType

TN = 2192
B0 = 64
HIST0 = 2112
S0 = 2128
GB = [0, 439, 878, 1317, 1755, 2192]
NGRP = 5
EPS = 1e-6
SB_BASE = 16512
SB_END = 229376


def build_program(dbg_stage=99, small=False, sub=0):
    nc = bass.Bass("TRN2", target_bir_lowering=False)

    def din(name, shape, dt=F32):
        return nc.dram_tensor(name, shape, dt, kind="ExternalInput").ap()

    def dout(name, shape, dt=F32):
        return nc.dram_tensor(name, shape, dt, kind="ExternalOutput").ap()

    xc = din("xc", [TN, 1024])
    cpool = din("cpool", [16, 1024])
    ck_in = din("ck_in", [2048, 1024])
    cv_in = din("cv_in", [2048, 1024])
    clf = din("clf", [128, 256])
    win = din("win", [4, 22, 128, 2048] if not small else [1, 1, 128, 2048])
    wout = din("wout", [4, 2, 8, 128, 1408] if not small else [1, 1, 1, 128, 1408])
    wpool = din("wpool", [128, 2048])
    wkv = din("wkv", [128, 8 * 2048])
    wfg = din("wfg", [128, 128])
    wq = din("wq", [16, 128, 512])
    wo = din("wo", [16, 64, 1024])
    gall = din("gall", [128, 72])
    bfg = din("bfg", [128, 16])
    ident_in = din("ident", [128, 128])
    tri_in = din("tri", [128, 128])
    invc_in = din("invc", [128, 64])
    mrank_in = din("mrank", [128, 4])
    pen_in = din("pen", [16, 128, 512])
    pens_in = din("pens", [128, 64])

    y_p = dout("y_p", [2048, 1024])
    y_s = dout("y_s", [64, 1024])
    pool_p = dout("pool_p", [16, 1024])
    pool_s = dout("pool_s", [16, 1024])
    k_p = dout("k_p", [2048, 1024])
    v_p = dout("v_p", [2048, 1024])
    lf_p = dout("lf_p", [2048, 16])
    k_s = dout("k_s", [64, 1024])
    v_s = dout("v_s", [64, 1024])
    lf_s = dout("lf_s", [64, 16])

    KTx = [nc.dram_tensor("KTx%d" % i, [128, 4096], BF16) for i in range(4)]
    KTall = [nc.dram_tensor("KTall%d" % i, [512, 4096], BF16) for i in range(4)]
    Vx = [nc.dram_tensor("Vx%d" % i, [128, 2560], BF16) for i in range(8)]
    Vall = [nc.dram_tensor("Vall%d" % i, [512, 2560], BF16) for i in range(8)]
    KTxv = [t.ap().rearrange("p (a c) -> (p a) c", a=2) for t in KTx]
    Vxv = [t.ap().rearrange("p (a c) -> (p a) c", a=2).rearrange("(h p) (l c) -> p h l c", p=128, c=80) for t in Vx]
    cw_x = nc.dram_tensor("cw_x", [128, 256], F32)
    cw_all = nc.dram_tensor("cw_all", [512, 256], F32)
    cq_d = nc.dram_tensor("cq_d", [16, 2112], BF16)

    S = Sched(nc, ring=16)
    GBm = list(GB)

    cur = [SB_BASE]

    def alloc(name, shape, dt, at=None):
        nbytes = int(np.prod(shape[1:])) * (4 if dt == F32 else 2)
        nbytes = (nbytes + 31) // 32 * 32
        if at is None:
            off = cur[0]
            cur[0] += nbytes
        else:
            off = at
        assert off + nbytes <= SB_END, (name, off, nbytes)
        return nc.alloc_sbuf_tensor_at(name, shape, dt, offset=off), off + nbytes

    xT, _ = alloc("xT", [128, 8, TN], F32)
    gal, _ = alloc("gal", [128, 72], F32)
    ident, _ = alloc("ident", [128, 128], F32)
    identb, _ = alloc("identb", [128, 128], BF16)
    tri, _ = alloc("tri", [128, 128], F32)
    onesb, _ = alloc("onesb", [128, 128], BF16)
    onesf, _ = alloc("onesf", [128, 128], F32)
    bfg_t, _ = alloc("bfg_t", [128, 16], F32)
    invc, _ = alloc("invc", [128, 4, 16], F32)
    mrank, _ = alloc("mrank", [128, 4], F32)
    epsb, _ = alloc("epsb", [128, 1], F32)
    one1, _ = alloc("one1", [128, 1], F32)
    KTs, _ = alloc("KTs", [64, 16, 64], BF16)
    Vsn, _ = alloc("Vsn", [128, 1024], BF16)
    lfall, _ = alloc("lfall", [128, 17, 16], F32)
    cwt, _ = alloc("cwt", [128, 17, 16], F32)
    cq, _ = alloc("cq", [128, 17, 16], F32)
    ncks, _ = alloc("ncks", [128, 17, 16], F32)
    upT, _ = alloc("upT", [128, 8, 32], F32)
    xnT, _ = alloc("xnT", [128, 8, TN], BF16)
    RSTD_OFF = cur[0]
    rstd, _ = alloc("rstd", [128, TN], F32)
    OV = cur[0]
    o = OV
    actT, o = alloc("actT", [128, 11, TN], BF16, at=o)
    win_t = []
    for i in range(3):
        t, o = alloc("win_t%d" % i, [128, 2, 8, 128], BF16, at=o)
        win_t.append(t)
    wout_t = []
    for i in range(2):
        t, o = alloc("wout_t%d" % i, [128, 11, 128], BF16, at=o)
        wout_t.append(t)
    wst = []
    for i in range(3):
        t, o = alloc("wst%d" % i, [128, 1024], F32, at=o)
        wst.append(t)
    sq = []
    for i in range(2):
        t, o = alloc("sq%d" % i, [128, 512], BF16, at=o)
        sq.append(t)
    sil = []
    for i in range(2):
        t, o = alloc("sil%d" % i, [128, 512], F32, at=o)
        sil.append(t)
    FFN_END = o
    o = OV
    xin = []
    for i in range(2):
        t, o = alloc("xin%d" % i, [128, 1024], F32, at=o)
        xin.append(t)
    o = OV
    ut, o = alloc("ut", [128, TN], F32, at=o)
    s1, o = alloc("s1", [128, TN], F32, at=o)
    s2, o = alloc("s2", [128, TN], F32, at=o)
    wpool_t, o = alloc("wpool_t", [128, 4, 2, 256], BF16, at=o)
    hist_in, o = alloc("hist_in", [16, 1024], F32, at=o)
    histT, o = alloc("histT", [128, 8, 16], F32, at=o)
    tmp16, o = alloc("tmp16", [128, 16], F32, at=o)
    pout, o = alloc("pout", [32, 1024], F32, at=o)
    o = OV
    wkv_t, o = alloc("wkv_t", [128, 8, 2048], BF16, at=o)
    wfg_t, o = alloc("wfg_t", [128, 8, 16], BF16, at=o)
    stage = []
    for i in range(2):
        t, o = alloc("stage%d" % i, [128, 2048], F32, at=o)
        stage.append(t)
    vst = []
    for i in range(2):
        t, o = alloc("vst%d" % i, [128, 16, 80], BF16, at=o)
        vst.append(t)
    kst = []
    for i in range(4):
        t, o = alloc("kst%d" % i, [128, 512], BF16, at=o)
        kst.append(t)
    zt = []
    for i in range(5):
        t, o = alloc("zt%d" % i, [128, 16], F32, at=o)
        zt.append(t)
    clf_t, o = alloc("clf_t", [128, 16, 16], F32, at=o)
    cwp, o = alloc("cwp", [128, 16, 16], F32, at=o)
    sc1, o = alloc("sc1", [128, 16, 16], F32, at=o)
    sc2, o = alloc("sc2", [128, 16, 16], F32, at=o)
    tots, o = alloc("tots", [128, 16, 16], F32, at=o)
    o = OV
    KTh, o = alloc("KTh", [128, 8192], BF16, at=o)
    Vh, o = alloc("Vh", [128, 64, 80], BF16, at=o)
    Qh, o = alloc("Qh", [128, 2112], BF16, at=o)
    pen_t, o = alloc("pen_t", [128, 16, 512], BF16, at=o)
    pens_t, o = alloc("pens_t", [128, 64], BF16, at=o)
    PT = []
    for i in range(4):
        t, o = alloc("PT%d" % i, [128, 512], BF16, at=o)
        PT.append(t)
    nck, o = alloc("nck", [128, 64, 16], F32, at=o)
    cwg, o = alloc("cwg", [128, 64, 16], F32, at=o)
    totb, o = alloc("totb", [128, 64, 16], F32, at=o)
    scA, o = alloc("scA", [128, 64, 16], F32, at=o)
    rd, o = alloc("rd", [65, 512], F32, at=o)
    rbs, o = alloc("rbs", [64, 512], F32, at=o)
    OTn, o = alloc("OTn", [64, 512], BF16, at=o)
    wq_t, o = alloc("wq_t", [128, 8, 64], BF16, at=o)
    wo_t, o = alloc("wo_t", [64, 1024], BF16, at=o)
    ckraw, o = alloc("ckraw", [128, 16, 64], BF16, at=o)
    cvh, o = alloc("cvh", [128, 16, 64], BF16, at=o)
    KTsh, o = alloc("KTsh", [128, 2112], BF16, at=o)
    PTs = []
    for i in range(2):
        t, o = alloc("PTs%d" % i, [128, 64], BF16, at=o)
        PTs.append(t)
    rs, o = alloc("rs", [64, 64], F32, at=o)
    OTs, o = alloc("OTs", [64, 64], BF16, at=o)
    cqT, _ = alloc("cqT", [16, 2112], BF16, at=RSTD_OFF)
    o = OV
    ytmp = []
    for i in range(2):
        t, o = alloc("ytmp%d" % i, [128, 128], F32, at=o)
        ytmp.append(t)
    ostage = []
    for i in range(2):
        t, o = alloc("ostage%d" % i, [128, 1024], F32, at=o)
        ostage.append(t)

    ps = nc.alloc_psum_tensor("ps", [128, 8, 512], F32)

    def bank(i):
        return ps[:, i, :]

    def P(i):
        return ("ps", i)

    def mm(out, lhsT, rhs, start, stop, reads, writes):
        S.op("pe", lambda e: e.matmul(out, lhsT=lhsT, rhs=rhs, start=start, stop=stop), reads, writes)

    def tr(out, in_, idn, reads, writes):
        S.op("pe", lambda e: e.transpose(out=out, in_=in_, identity=idn), reads, writes)

    def act(out, in_, func, reads, writes, bias=None, scale=1.0):
        if bias is None:
            S.op("act", lambda e: e.activation(out=out, in_=in_, func=func, scale=scale), reads, writes)
        else:
            S.op("act", lambda e: e.activation(out=out, in_=in_, func=func, bias=bias, scale=scale), reads, writes)

    def cp(eng, out, in_, reads, writes):
        if eng == "act":
            S.op("act", lambda e: e.copy(out=out, in_=in_), reads, writes)
        else:
            S.op(eng, lambda e: e.tensor_copy(out=out, in_=in_), reads, writes)

    def stt(eng, out, in0, scalar, in1, op0, op1, reads, writes):
        S.op(eng, lambda e: e.scalar_tensor_tensor(out=out, in0=in0, scalar=scalar, in1=in1, op0=op0, op1=op1),
             reads, writes)

    def tt(eng, out, in0, in1, op, reads, writes):
        S.op(eng, lambda e: e.tensor_tensor(out=out, in0=in0, in1=in1, op=op), reads, writes)

    def ts(eng, out, in0, s1_, s2_, op0, op1, reads, writes):
        S.op(eng, lambda e: e.tensor_scalar(out=out, in0=in0, scalar1=s1_, scalar2=s2_, op0=op0, op1=op1),
             reads, writes)

    def tss(eng, out, in0, s1_, op0, reads, writes):
        S.op(eng, lambda e: e.tensor_single_scalar(out=out, in_=in0, scalar=s1_, op=op0), reads, writes)

    def memset(eng, ap, val, writes):
        S.op(eng, lambda e: e.memset(ap, val), (), writes)

    def dma(q, out, in_, reads, writes):
        S.dma(q, lambda e: e.dma_start(out=out, in_=in_), reads, writes)

    XT = [("xT", k) for k in range(8)]
    XN = [("xn", k) for k in range(8)]

    dma("sp", gal[:, :], gall[:, :], (), ["gal"])
    dma("sp", ident[:, :], ident_in[:, :], (), ["ident"])
    dma("sp", tri[:, :], tri_in[:, :], (), ["tri"])
    dma("sp", bfg_t[:, :], bfg[:, :], (), ["bfg"])
    dma("sp", invc[:, :, :], invc_in.rearrange("p (g c) -> p g c", g=4), (), ["invc"])
    dma("sp", mrank[:, :], mrank_in[:, :], (), ["mrank"])
    memset("dve", onesb[:, :], 1.0, ["onesb"])
    memset("dve", onesf[:, :], 1.0, ["onesf"])
    memset("dve", epsb[:, :], EPS, ["epsb"])
    memset("dve", one1[:, :], 1.0, ["one1"])
    memset("dve", lfall[:, :, :], 0.0, ["lfall"])
    cp("dve", identb[:, :], ident[:, :], ["ident"], ["identb"])

    for t in range(18):
        r0 = 128 * t
        nr = min(128, TN - r0)
        xb = xin[t % 2]
        kx = ("xin", t % 2)
        dma("sp", xb[0:nr, :], xc[r0:r0 + nr, :], (), [kx])
        for half in range(2):
            bk = 6 + half
            for i in range(4):
                c = 4 * half + i
                tr(bank(bk)[:, i * 128:i * 128 + nr], xb[0:nr, c * 128:(c + 1) * 128], ident[0:nr, 0:nr],
                   [kx, "ident"], [P(bk)])
            src = bank(bk).rearrange("p (a b) -> p a b", a=4)[:, :, 0:nr]
            cp("act" if half == 0 else "dve", xT[:, 4 * half:4 * half + 4, r0:r0 + nr], src, [P(bk)],
               [("xT", 4 * half + i) for i in range(4)])

    S.barrier()
    def norm_stats_g(g):
        c0, c1 = GBm[g], GBm[g + 1]
        n = c1 - c0
        for kc in range(8):
            sb_ = sq[kc % 2]
            act(sb_[:, 0:n], xT[:, kc, c0:c1], AF.Square, [("xT", kc)], [("sq", kc % 2)])
            mm(bank(6)[:, 0:n], onesb[:, :], sb_[:, 0:n], kc == 0, kc == 7, [("sq", kc % 2), "onesb"], [P(6)])
        act(rstd[:, c0:c1], bank(6)[:, 0:n], AF.Sqrt, [P(6), "epsb"], [("rstd", g)], bias=epsb[:, 0:1], scale=1.0 / 1024)
        S.op("dve", lambda e, a=rstd[:, c0:c1]: e.reciprocal(out=a, in_=a), [("rstd", g)], [("rstd", g), "rstd"])

    def norm_stats():
        for g in range(NGRP):
            norm_stats_g(g)

    def norm_apply_g(gi, g):
        c0, c1 = GBm[g], GBm[g + 1]
        for kc in range(8):
            stt("dve", xnT[:, kc, c0:c1], xT[:, kc, c0:c1], gal[:, gi * 8 + kc:gi * 8 + kc + 1], rstd[:, c0:c1],
                ALU.mult, ALU.mult, [("xT", kc), ("rstd", g), "gal"], [("xn", kc)])

    def norm_apply(gi):
        pass

    def norm_full(gi):
        for g in range(NGRP):
            norm_stats_g(g)
            norm_apply_g(gi, g)

    wcnt = {"in": 0, "out": 0, "st": 0}

    def wload(dst3, src2, nk, wkey):
        sl = wcnt["st"] % 3
        wcnt["st"] += 1
        stg = wst[sl]
        dma("sp", stg[:, 0:nk * 128], src2, (), [("wst", sl)])
        cp("pool", dst3, stg[:, 0:nk * 128].rearrange("p (k n) -> p k n", k=nk), [("wst", sl)], [wkey])


    def ffn(f):
        for half in range(2):
            for ci in range(11):
                ffc = half * 11 + ci
                slot = wcnt["in"] % 3
                wcnt["in"] += 1
                wt = win_t[slot]
                wload(wt[:, 0, :, :], win[f, ffc][:, 0:1024], 8, ("win", slot))
                wload(wt[:, 1, :, :], win[f, ffc][:, 1024:2048], 8, ("win", slot))
                for g in range(NGRP):
                    c0, c1 = GBm[g], GBm[g + 1]
                    n = c1 - c0
                    bg = g % 2
                    bu = 2 + g % 2
                    for kc in range(8):
                        mm(bank(bg)[:, 0:n], wt[:, 0, kc, :], xnT[:, kc, c0:c1], kc == 0, kc == 7,
                           [("win", slot), ("xn", kc)], [P(bg)])
                    for kc in range(8):
                        mm(bank(bu)[:, 0:n], wt[:, 1, kc, :], xnT[:, kc, c0:c1], kc == 0, kc == 7,
                           [("win", slot), ("xn", kc)], [P(bu)])
                    st = sil[g % 2]
                    act(st[:, 0:n], bank(bg)[:, 0:n], AF.Silu, [P(bg)], [("sil", g % 2)])
                    tt("dve", actT[:, ci, c0:c1], st[:, 0:n], bank(bu)[:, 0:n], ALU.mult,
                       [("sil", g % 2), P(bu)], [("act", ci)])
            for oc in range(8):
                slot = wcnt["out"] % 2
                wcnt["out"] += 1
                wt = wout_t[slot]
                wload(wt[:, 0:8, :], wout[f, half, oc][:, 0:1024], 8, ("wout", slot))
                wload(wt[:, 8:11, :], wout[f, half, oc][:, 1024:1408], 3, ("wout", slot))
                for g in range(NGRP):
                    c0, c1 = GBm[g], GBm[g + 1]
                    n = c1 - c0
                    by = 4 + g % 2
                    for k in range(11):
                        mm(bank(by)[:, 0:n], wt[:, k, :], actT[:, k, c0:c1], k == 0, k == 10,
                           [("wout", slot), ("act", k)], [P(by)])
                    stt("dve", xT[:, oc, c0:c1], bank(by)[:, 0:n], 0.5, xT[:, oc, c0:c1], ALU.mult, ALU.add,
                        [P(by), ("xT", oc)], [("xT", oc)])

    def final_out(raw):
        S.barrier()
        if not raw:
            norm_stats()
        for (c0, nt, wi) in windows_f:
            og = ostage[wi % 2]
            ko = ("ostage", wi % 2)
            for kc in range(8):
                yt = ytmp[kc % 2]
                ky = ("ytmp", kc % 2)
                if raw:
                    cp("dve", yt[:, 0:nt], xT[:, kc, c0:c0 + nt], [("xT", kc)], [ky])
                else:
                    stt("dve", yt[:, 0:nt], xT[:, kc, c0:c0 + nt], gal[:, 56 + kc:57 + kc], rstd[:, c0:c0 + nt], ALU.mult, ALU.mult,
                        [("xT", kc), "rstd", "gal"], [ky])
                bk = 6 + kc // 4
                tr(bank(bk)[0:nt, (kc % 4) * 128:(kc % 4 + 1) * 128], yt[:, 0:nt], ident[:, :], [ky, "ident"], [P(bk)])
            cp("act", og[0:nt, 0:512], bank(6)[0:nt, :], [P(6)], [ko])
            cp("act", og[0:nt, 512:1024], bank(7)[0:nt, :], [P(7)], [ko])
            if wi < 16:
                dma("sp", y_p[wi * 128:(wi + 1) * 128, :], og[:, :], [ko], ["y_p"])
            else:
                dma("sp", y_s[:, :], og[0:64, :], [ko], ["y_s"])
        S.final_wait_all("sp")
        with ExitStack() as st:
            block = st.enter_context(nc.Block())
            S.emit(block, st)
        return nc

    windows_f = [(64 + 128 * lt, 128, lt) for lt in range(16)] + [(S0, 64, 16)]
    if dbg_stage == 0:
        return final_out(True)
    if small:
        def ffn(f):
            pass
    norm_full(0)
    ffn(0)

    if dbg_stage == 1:
        return final_out(True)
    S.barrier()
    norm_stats()
    dma("pool", wpool_t[:, :, :, :], wpool.rearrange("p (g k n) -> p g k n", g=4, k=2), (), ["wpool"])
    dma("sp", hist_in[:, :], cpool[:, :], (), ["hist_in"])
    for kc in range(8):
        tr(bank(7)[:, kc * 16:(kc + 1) * 16], hist_in[0:16, kc * 128:(kc + 1) * 128], ident[0:16, 0:16],
           ["hist_in", "ident"], [P(7)])
    cp("act", histT[:, :, :], bank(7)[:, 0:128].rearrange("p (a b) -> p a b", a=8), [P(7)], ["histT"])

    def useg(t_):
        v = t_[:, 0:2112].rearrange("p (m c) -> p m c", c=528)
        return v[:, :, 0:16], v[:, :, 16:528]

    for kc in range(8):
        grp = kc // 2
        w = 2 ** (grp + 1)
        gsc = gal[:, 8 + kc:8 + kc + 1]
        uh, ub = useg(ut)
        rh = rstd[:, 0:64].rearrange("p (m c) -> p m c", c=16)
        rb = rstd[:, 64:2112].rearrange("p (m c) -> p m c", c=512)
        xh = xT[:, kc, 0:64].rearrange("p (m c) -> p m c", c=16)
        xbk = xT[:, kc, 64:2112].rearrange("p (m c) -> p m c", c=512)
        stt("dve", uh, xh, gsc, rh, ALU.mult, ALU.mult, [("xT", kc), "rstd", "gal"], ["ut"])
        stt("dve", ub, xbk, gsc, rb, ALU.mult, ALU.mult, [("xT", kc), "rstd", "gal"], ["ut"])
        stt("dve", ut[:, S0:TN], xT[:, kc, S0:TN], gsc, rstd[:, S0:TN], ALU.mult, ALU.mult,
            [("xT", kc), "rstd", "gal"], ["ut"])
        cp("pool", ut[:, HIST0:S0], histT[:, kc, :], ["histT"], ["ut"])
        cp("act", upT[:, kc, 0:16], ut[:, 2096:2112], ["ut"], ["upT"])
        cp("act", upT[:, kc, 16:32], ut[:, 2176:2192], ["ut"], ["upT"])
        prev, pk = ut, "ut"
        pp = [(s1, "s1"), (s2, "s2")]
        for j in range(grp + 1):
            sh = 2 ** j
            nt_, nk_ = pp[j % 2]
            cp("dve", nt_[:, 0:sh], prev[:, 0:sh], [pk], [nk_])
            tt("dve", nt_[:, sh:TN], prev[:, sh:TN], prev[:, 0:TN - sh], ALU.add, [pk], [nk_])
            prev, pk = nt_, nk_
        _, sbv = useg(prev)
        dxb = xnT[:, kc, 64:2112].rearrange("p (m c) -> p m c", c=512)
        stt("dve", dxb, sbv, 1.0 / w, ub, ALU.mult, ALU.subtract, [pk, "ut"], [("xn", kc)])
        stt("dve", xnT[:, kc, S0:TN], prev[:, S0:TN], 1.0 / w, ut[:, S0:TN], ALU.mult, ALU.subtract,
            [pk, "ut"], [("xn", kc)])
        tt("dve", tmp16[:, :], prev[:, 16:32], invc[:, grp, :], ALU.mult, [pk, "invc"], ["tmp16"])
        tt("dve", xnT[:, kc, 64:80], tmp16[:, :], ut[:, 16:32], ALU.subtract, ["tmp16", "ut", ("xn", kc)], [("xn", kc)])
    colgroups = [(64 + 512 * m, 64 + 512 * (m + 1)) for m in range(4)] + [(S0, TN)]
    ib = 0
    for (c0, c1) in colgroups:
        n = c1 - c0
        for grp in range(4):
            for oc2 in range(2):
                bk = ib % 2
                ib += 1
                for k2 in range(2):
                    mm(bank(bk)[:, 0:n], wpool_t[:, grp, k2, oc2 * 128:(oc2 + 1) * 128], xnT[:, 2 * grp + k2, c0:c1],
                       k2 == 0, k2 == 1, ["wpool", ("xn", 2 * grp + k2)], [P(bk)])
                oc = 2 * grp + oc2
                stt("dve", xT[:, oc, c0:c1], bank(bk)[:, 0:n], gal[:, 64 + oc:64 + oc + 1], xT[:, oc, c0:c1],
                    ALU.mult, ALU.add, [P(bk), ("xT", oc), "gal"], [("xT", oc)])
    for kc in range(8):
        bk = 6 + kc // 4
        tr(bank(bk)[0:32, (kc % 4) * 128:(kc % 4 + 1) * 128], upT[:, kc, :], ident[:, :], ["upT", "ident"], [P(bk)])
    cp("act", pout[:, 0:512], bank(6)[0:32, :], [P(6)], ["pout"])
    cp("act", pout[:, 512:1024], bank(7)[0:32, :], [P(7)], ["pout"])
    dma("sp", pool_p[:, :], pout[0:16, :], ["pout"], ["pool_p"])
    dma("sp", pool_s[:, :], pout[16:32, :], ["pout"], ["pool_s"])

    if dbg_stage == 2:
        return final_out(True)
    GBm[:] = [64, 490, 916, 1342, 1768, 2192]
    S.barrier()
    norm_full(2)
    ffn(1)

    if dbg_stage == 3:
        return final_out(True)
    S.barrier()
    norm_full(3)
    for hf in range(2):
        for kc_ in range(8):
            wload(wkv_t[:, kc_, hf * 1024:(hf + 1) * 1024].rearrange("p (k n) -> p k n", k=8),
                  wkv[:, kc_ * 2048 + hf * 1024:kc_ * 2048 + (hf + 1) * 1024], 8, "wkv")
    dma("pool", wfg_t[:, :, :], wfg.rearrange("p (k n) -> p k n", k=8), (), ["wfg"])
    for i in range(2):
        if sub & 4:
            continue
        memset("pool", vst[i][:, :, :], 0.0, [("vst", i)])
        memset("pool", vst[i][:, :, 64:65], 1.0, [("vst", i)])
    windows = [(64 + 128 * lt, 128, lt) for lt in range(16)] + [(S0, 64, 16)]
    for (c0, nt, wi) in windows:
        if sub & 8:
            continue
        if (sub & 64) and wi >= 2:
            continue
        sg = stage[wi % 2]
        ksg = ("stage", wi % 2)
        vt = vst[wi % 2]
        kvt = ("vst", wi % 2)
        for cb in range(4):
            bk = cb % 2
            for kc in range(8):
                mm(bank(bk)[0:nt, :], xnT[:, kc, c0:c0 + nt], wkv_t[:, kc, cb * 512:(cb + 1) * 512], kc == 0, kc == 7,
                   [("xn", kc), "wkv"], [P(bk)])
            cp("act", sg[0:nt, cb * 512:(cb + 1) * 512], bank(bk)[0:nt, :], [P(bk)], [ksg])
            if cb >= 2 and not (sub & 16):
                srcv = sg[0:nt, cb * 512:(cb + 1) * 512].rearrange("p (h d) -> p h d", d=64)
                if wi < 16:
                    cp("dve", vt[0:nt, (cb - 2) * 8:(cb - 1) * 8, 0:64], srcv, [ksg], [kvt])
                else:
                    cp("dve", Vsn[0:nt, (cb - 2) * 512:(cb - 1) * 512], sg[0:nt, cb * 512:(cb + 1) * 512], [ksg], ["Vsn"])
        if wi < 16:
            if not (sub & 32):
                dma("sp", k_p[wi * 128:(wi + 1) * 128, :], sg[:, 0:1024], [ksg], ["k_p"])
                dma("sp", v_p[wi * 128:(wi + 1) * 128, :], sg[:, 1024:2048], [ksg], ["v_p"])
            for j in range(8):
                if sub & 2:
                    continue
                dma("sp", Vxv[j][:, :, wi, :], vt[:, 2 * j:2 * j + 2, :], [kvt], [("V_x", j)])
        else:
            dma("sp", k_s[:, :], sg[0:64, 0:1024], [ksg], ["k_s"])
            dma("sp", v_s[:, :], sg[0:64, 1024:2048], [ksg], ["v_s"])
        if sub & 1:
            continue
        for kc in range(8):
            mm(bank(2)[0:nt, 0:16], xnT[:, kc, c0:c0 + nt], wfg_t[:, kc, :], kc == 0, kc == 7, [("xn", kc), "wfg"], [P(2)])
        z, nz, e_, l_, m_ = [zt[i][0:nt, :] for i in range(5)]
        tt("dve", z, bank(2)[0:nt, 0:16], bfg_t[0:nt, :], ALU.add, [P(2), "bfg"], ["z0"])
        S.op("dve", lambda e, a=nz, b=z: e.tensor_scalar_mul(out=a, in0=b, scalar1=-1.0), ["z0"], ["z1"])
        tt("dve", nz, z, nz, ALU.min, ["z0", "z1"], ["z1"])
        act(e_, nz, AF.Exp, ["z1"], ["z2"])
        act(l_, e_, AF.Ln, ["z2", "one1"], ["z3"], bias=one1[0:nt, 0:1])
        S.op("dve", lambda e, a=m_, b=z: e.tensor_scalar_min(out=a, in0=b, scalar1=0.0), ["z0"], ["z4"])
        tt("dve", lfall[0:nt, wi, :], m_, l_, ALU.subtract, ["z4", "z3"], ["lfall"])
    for wi_ in range(16):
        dma("sp", lf_p[wi_ * 128:(wi_ + 1) * 128, :], lfall[:, wi_, :], ["lfall"], ["lf_p"])
    dma("sp", lf_s[:, :], lfall[0:64, 16, :], ["lfall"], ["lf_s"])
    if dbg_stage == 31:
        return final_out(True)
    ik = 0
    for hp in range(8):
        for gi_, (c0, c1) in enumerate(colgroups[0:4]):
            n = c1 - c0
            bk = ik % 2
            kb = kst[ik % 4]
            kk = ("kst", ik % 4)
            ik += 1
            for kc in range(8):
                mm(bank(bk)[:, 0:n], wkv_t[:, kc, hp * 128:(hp + 1) * 128], xnT[:, kc, c0:c1], kc == 0, kc == 7,
                   ["wkv", ("xn", kc)], [P(bk)])
            cp("act" if ik % 2 else "dve", kb[:, 0:n], bank(bk)[:, 0:n], [P(bk)], [kk])
            i4 = (2 * hp) // 4
            r0_ = ((2 * hp) % 4) * 64
            dma("sp", KTxv[i4][r0_:r0_ + 128, gi_ * 512:(gi_ + 1) * 512], kb[:, 0:n], [kk], [("KT_x", i4)])
    (c0, c1) = colgroups[4]
    for h in range(16):
        bk = 2 + h % 2
        for kc in range(8):
            mm(bank(bk)[0:64, 0:64], wkv_t[:, kc, h * 64:(h + 1) * 64], xnT[:, kc, c0:c1], kc == 0, kc == 7,
               ["wkv", ("xn", kc)], [P(bk)])
        cp("dve", KTs[:, h, :], bank(bk)[0:64, 0:64], [P(bk)], ["KTs"])
    if dbg_stage == 32:
        return final_out(True)
    mm(bank(3)[:, 0:272], tri[:, :], lfall[:, :, :].rearrange("p a b -> p (a b)"), True, True, ["tri", "lfall"], [P(3)])
    cp("dve", cwt[:, :, :].rearrange("p a b -> p (a b)"), bank(3)[:, 0:272], [P(3)], ["cwt"])
    dma("sp", cw_x.ap()[:, :], cwt[:, 0:16, :].rearrange("p a b -> p (a b)"), ["cwt"], ["cw_x"])
    if dbg_stage == 33:
        return final_out(True)
    RG = [[0, 1, 2, 3], [4, 5, 6, 7]]
    S.cc(lambda e: e.collective_compute("AllGather", ALU.bypass, replica_groups=RG, ins=[cw_x.ap().opt()],
                                        outs=[cw_all.ap().opt()]), ["cw_x"], ["cw_all"])
    for i in range(4):
        S.cc(lambda e, a=KTx[i], c=KTall[i]: e.collective_compute("AllGather", ALU.bypass, replica_groups=RG, ins=[a.ap().opt()],
                                                                  outs=[c.ap().opt()]), [("KT_x", i)], [("KT_all", i)])
    for j in range(8):
        S.cc(lambda e, a=Vx[j], c=Vall[j]: e.collective_compute("AllGather", ALU.bypass, replica_groups=RG, ins=[a.ap().opt()],
                                                                outs=[c.ap().opt()]), [("V_x", j)], [("V_all", j)])
    if dbg_stage == 34:
        return final_out(True)
    dma("sp", clf_t[:, :, :], clf.rearrange("p (l h) -> p l h", l=16), (), ["clf"])
    clf2 = clf_t[:, :, :].rearrange("p a b -> p (a b)")
    mm(bank(4)[:, 0:256], tri[:, :], clf2, True, True, ["tri", "clf"], [P(4)])
    mm(bank(5)[:, 0:256], onesf[:, :], clf2, True, True, ["onesf", "clf"], [P(5)])
    cp("dve", cwp[:, :, :].rearrange("p a b -> p (a b)"), bank(4)[:, 0:256], [P(4)], ["cwp"])
    cp("dve", tots[:, :, :].rearrange("p a b -> p (a b)"), bank(5)[:, 0:256], [P(5)], ["tots"])
    prev, pk = tots, "tots"
    pp = [(sc1, "sc1"), (sc2, "sc2")]
    for j in range(4):
        sh = 2 ** j
        nt_, nk_ = pp[j % 2]
        cp("dve", nt_[:, 0:sh, :], prev[:, 0:sh, :], [pk], [nk_])
        tt("dve", nt_[:, sh:16, :], prev[:, sh:16, :], prev[:, 0:16 - sh, :], ALU.add, [pk], [nk_])
        prev, pk = nt_, nk_
    incl, ik_ = prev, pk
    tt("dve", cq[:, 0:16, :], incl[:, :, :], tots[:, :, :], ALU.subtract, [ik_, "tots"], ["cqtmp"])
    tt("dve", cq[:, 0:16, :], cq[:, 0:16, :], cwp[:, :, :], ALU.add, ["cqtmp", "cwp"], ["cqtmp"])
    S.op("dve", lambda e: e.tensor_scalar_mul(out=ncks[:, 0:16, :], in0=cq[:, 0:16, :], scalar1=-1.0), ["cqtmp"], ["ncks"])
    tt("dve", cq[:, 16, :], cwt[:, 16, :], incl[:, 15, :], ALU.add, ["cwt", ik_, "cqtmp"], ["cq"])
    S.op("dve", lambda e: e.tensor_scalar_mul(out=ncks[:, 16, :], in0=cq[:, 16, :], scalar1=-1.0), ["cq"], ["ncks"])

    if dbg_stage == 4:
        return final_out(True)
    S.barrier()
    norm_full(4)
    ffn(2)
    norm_full(5)
    S.barrier()

    if dbg_stage == 5:
        return final_out(True)
    cwav = cw_all.ap().rearrange("(r p) (m c) -> r p m c", p=128, m=4)
    cwgv = cwg[:, :, :].rearrange("p (m r w) h -> p m r (w h)", m=4, r=4)
    totv = totb[:, :, :].rearrange("p (m r w) h -> p m r (w h)", m=4, r=4)
    for r in range(4):
        dma("sp", cwgv[:, :, r, :], cwav[r], ["cw_all"], ["cwg"])
        for m in range(4):
            dma("sp", totv[:, m, r, :], cw_all.ap()[r * 128 + 127:r * 128 + 128, m * 64:(m + 1) * 64].to_broadcast([128, 64]),
                ["cw_all"], ["totb"])
    prev, pk = totb, "totb"
    pp = [(scA, "scA"), (nck, "nck")]
    for j in range(6):
        sh = 2 ** j
        nt_, nk_ = pp[j % 2]
        cp("dve", nt_[:, 0:sh, :], prev[:, 0:sh, :], [pk], [nk_])
        tt("dve", nt_[:, sh:64, :], prev[:, sh:64, :], prev[:, 0:64 - sh, :], ALU.add, [pk], [nk_])
        prev, pk = nt_, nk_
    tt("dve", scA[:, :, :], nck[:, :, :], totb[:, :, :], ALU.subtract, ["nck", "totb"], ["scA"])
    tt("dve", scA[:, :, :], scA[:, :, :], cwg[:, :, :], ALU.add, ["scA", "cwg"], ["scA"])
    S.op("dve", lambda e: e.tensor_scalar_mul(out=nck[:, :, :], in0=scA[:, :, :], scalar1=-1.0), ["scA"], ["nck"])
    ckv = scA[:, :, :].rearrange("p (m r w) h -> p m r (w h)", m=4, r=4)
    cqv = cq[:, 0:16, :].rearrange("p (m w) h -> p m (w h)", m=4)
    S.op("dve", lambda e: e.tensor_scalar_mul(out=cqv, in0=ckv[:, :, 0, :], scalar1=mrank[:, 0:1]), ["scA", "mrank", "cq"], ["cq"])
    for r in range(1, 4):
        stt("dve", cqv, ckv[:, :, r, :], mrank[:, r:r + 1], cqv, ALU.mult, ALU.add, ["scA", "mrank", "cq"], ["cq"])
    for rnd in range(5):
        nl = 4 if rnd < 4 else 1
        for i in range(nl):
            lt = rnd * 4 + i
            nt = 128 if lt < 16 else 64
            tr(bank(7)[0:16, i * 128:i * 128 + nt], cq[0:nt, lt, :], ident[0:nt, 0:nt], ["cq", "ident"], [P(7)])
        ncols = 512 if rnd < 4 else 64
        cp("act", cqT[:, rnd * 512:rnd * 512 + ncols], bank(7)[0:16, 0:ncols], [P(7)], ["cqT"])
    dma("sp", cq_d.ap()[:, :], cqT[:, :], ["cqT"], ["cq_d"])

    dma("pool", pen_t[:, :, :], pen_in.rearrange("k p j -> p k j"), (), ["pen"])
    dma("pool", pens_t[:, :], pens_in[:, :], (), ["pens"])
    memset("dve", KTh[64:128, :], 0.0, ["KTh"])
    memset("dve", KTsh[64:128, :], 0.0, ["KTsh"])
    memset("dve", Qh[64:128, :], 0.0, ["Qh"])
    memset("dve", KTh[64:65, :], 1.0, ["KTh"])
    memset("dve", KTsh[64:65, :], 1.0, ["KTsh"])
    KTav = [t.ap().rearrange("p (a c) -> (p a) c", a=2).rearrange("(r q) (m c) -> r q m c", r=4, m=4) for t in KTall]
    KThv = KTh[0:64, :].rearrange("p (m r c) -> p m r c", m=4, r=4)
    Vav = [t.ap().rearrange("p (a c) -> (p a) c", a=2).rearrange("(r q) (m c) -> r q m c", r=4, m=4) for t in Vall]
    Vhv = Vh[:, :, :].rearrange("p (m r w) c -> p m r (w c)", m=4, r=4)
    ckv_in = ck_in.rearrange("(l p) (h d) -> p l h d", p=128, d=64)
    cvv_in = cv_in.rearrange("(l p) (h d) -> p l h d", p=128, d=64)
    for h in range(16):
        dma("pool", wq_t[:, :, :], wq[h].rearrange("p (k n) -> p k n", k=8), (), ["wq"])
        dma("pool", wo_t[:, :], wo[h], (), ["wo"])
        for r in range(4):
            dma("sp", KThv[:, :, r, :], KTav[h // 4][r, (h % 4) * 64:(h % 4 + 1) * 64], [("KT_all", h // 4)], ["KTh"])
            dma("sp", Vhv[:, :, r, :], Vav[h // 2][r, (h % 2) * 128:(h % 2 + 1) * 128], [("V_all", h // 2)], ["Vh"])
        dma("pool", ckraw[:, :, :], ckv_in[:, :, h, :], (), ["ckraw"])
        dma("pool", cvh[:, :, :], cvv_in[:, :, h, :], (), ["cvh"])
        for gi_, (c0, c1) in enumerate(colgroups):
            n = c1 - c0
            for kc in range(8):
                mm(bank(7)[0:64, 0:n], wq_t[:, kc, :], xnT[:, kc, c0:c1], kc == 0, kc == 7, ["wq", ("xn", kc)], [P(7)])
            S.op("act", lambda e, a=Qh[0:64, gi_ * 512:gi_ * 512 + n], b=bank(7)[0:64, 0:n]: e.mul(out=a, in_=b, mul=0.125),
                 [P(7)], ["Qh"])
        dma("sp", Qh[64:65, :], cq_d.ap()[h:h + 1, :], ["cq_d"], ["Qh"])
        it = 0
        pending = []
        for m in range(4):
            nkt = 16 * (m + 1)
            qcols = Qh[:, m * 512:(m + 1) * 512]
            bo = 2 + m % 2
            SB_ = (0, 1, 5, 7)

            def s_mm(kt):
                bs_ = SB_[kt % 4]
                diag_ = kt >= 16 * m
                mm(bank(bs_), KTh[:, kt * 128:(kt + 1) * 128], qcols, True, not diag_, ["KTh", "Qh"], [P(bs_)])
                if diag_:
                    mm(bank(bs_), identb[:, :], pen_t[:, kt - 16 * m, :], False, True, ["identb", "pen"], [P(bs_)])
            for k0 in range(4):
                s_mm(k0)
            for j in range(nkt // 2):
                ka, kb = 2 * j, 2 * j + 1
                act(PT[ka % 4][:, :], bank(SB_[ka % 4]), AF.Exp, [P(SB_[ka % 4]), P(SB_[kb % 4]), "nck"], [("PT", ka % 4)],
                    bias=nck[:, ka, h:h + 1])
                act(PT[kb % 4][:, :], bank(SB_[kb % 4]), AF.Exp, [P(SB_[kb % 4]), "nck"], [("PT", kb % 4)],
                    bias=nck[:, kb, h:h + 1])
                mm(bank(bo)[0:65, :], Vh[:, ka, 0:65], PT[ka % 4][:, :], ka == 0, False,
                   ["Vh", ("PT", ka % 4), ("PT", kb % 4)], [P(bo)])
                mm(bank(bo)[0:65, :], Vh[:, kb, 0:65], PT[kb % 4][:, :], False, kb == nkt - 1,
                   ["Vh", ("PT", kb % 4)], [P(bo)])
                if ka + 4 < nkt:
                    s_mm(ka + 4)
                    s_mm(kb + 4)
                if j >= 2 and pending:
                    pending.pop(0)()

            def make_epi(m_=m, bo_=bo):
                def st_a():
                    cp("act", rd[64:65, :], bank(bo_)[64:65, :], [P(bo_)], ["rd"])
                    S.op("dve", lambda e: e.reciprocal(out=rd[64:65, :], in_=rd[64:65, :]), ["rd"], ["rd"])

                def st_b():
                    mm(bank(6)[0:64, :], onesf[64:65, 0:64], rd[64:65, :], True, True, ["onesf", "rd"], [P(6)])
                    cp("act", rbs[:, :], bank(6)[0:64, :], [P(6)], ["rbs"])
                    tt("dve", OTn[:, :], bank(bo_)[0:64, :], rbs[:, :], ALU.mult, [P(bo_), "rbs"], ["OTn"])

                def st_c(oc):
                    def f():
                        by = (4, 6)[oc % 2]
                        mm(bank(by), wo_t[:, oc * 128:(oc + 1) * 128], OTn[:, :], True, True, ["wo", "OTn"], [P(by)])
                        c0_ = 64 + 512 * m_
                        tt("dve", xT[:, oc, c0_:c0_ + 512], bank(by), xT[:, oc, c0_:c0_ + 512], ALU.add,
                           [P(by), ("xT", oc)], [("xT", oc)])
                    return f
                return [st_a, (lambda: None), (lambda: None), (lambda: None), st_b] + [st_c(oc) for oc in range(8)]
            assert not pending
            pending.extend(make_epi())
        while pending:
            pending.pop(0)()
        bb = bank(7).bitcast(BF16)
        for rnd in range(2):
            for i in range(8):
                lt = rnd * 8 + i
                tr(bb[0:64, i * 128:(i + 1) * 128], ckraw[:, lt, :], identb[:, :], ["ckraw", "identb"], [P(7)])
            cp("dve", KTsh[0:64, rnd * 1024:(rnd + 1) * 1024], bb[0:64, :], [P(7)], ["KTsh"])
        cp("dve", KTsh[0:64, 2048:2112], KTs[:, h, :], ["KTs"], ["KTsh"])
        qs = Qh[:, 2048:2112]
        def ss_mm(lt):
            nk_ = 128 if lt < 16 else 64
            bs_ = lt % 2
            mm(bank(bs_)[0:nk_, 0:64], KTsh[:, lt * 128:lt * 128 + nk_], qs, True, lt < 16, ["KTsh", "Qh"], [P(bs_)])
            if lt == 16:
                mm(bank(bs_)[0:nk_, 0:64], identb[0:64, 0:64], pens_t[0:64, :], False, True, ["identb", "pens"], [P(bs_)])
        ss_mm(0)
        for lt in range(17):
            nk = 128 if lt < 16 else 64
            bs = lt % 2
            pt = PTs[lt % 2]
            kpt = ("PTs", lt % 2)
            if lt + 1 < 17:
                ss_mm(lt + 1)
            act(pt[0:nk, :], bank(bs)[0:nk, 0:64], AF.Exp, [P(bs), "ncks"], [kpt], bias=ncks[0:nk, lt, h:h + 1])
            vl = cvh[:, lt, :] if lt < 16 else Vsn[0:64, h * 64:(h + 1) * 64]
            mm(bank(2)[0:64, 0:64], vl, pt[0:nk, :], lt == 0, lt == 16, ["cvh", "Vsn", kpt], [P(2)])
            mm(bank(3)[0:64, 0:64], onesb[0:nk, 0:64], pt[0:nk, :], lt == 0, lt == 16, ["onesb", kpt], [P(3)])
        S.op("dve", lambda e: e.reciprocal(out=rs[:, :], in_=bank(3)[0:64, 0:64]), [P(3)], ["rs"])
        tt("dve", OTs[:, :], bank(2)[0:64, 0:64], rs[:, :], ALU.mult, [P(2), "rs"], ["OTs"])
        for oc in range(8):
            mm(bank(4)[:, oc * 64:(oc + 1) * 64], wo_t[:, oc * 128:(oc + 1) * 128], OTs[:, :], True, True, ["wo", "OTs"], [P(4)])
        tt("dve", xT[:, :, S0:TN], bank(4).rearrange("p (a b) -> p a b", a=8), xT[:, :, S0:TN], ALU.add,
           [P(4)] + XT, XT)

    if dbg_stage == 6:
        return final_out(True)
    S.barrier()
    norm_full(6)
    ffn(3)

    return final_out(False)


_NC_CACHE = {}


def kernel(x_prompt, x_sample, cache_pool, cache_k, cache_v, cache_logf,
           ln_ffn1, ln_mix, ln_ffn2, w_ffn_in, w_ffn_out, w_pool, pool_scale,
           ln_kv, w_kv, w_fgate, b_fgate, w_q, w_o, ln_final):
    f32 = np.float32
    A = lambda a: np.ascontiguousarray(np.asarray(a, dtype=f32))
    x_prompt, x_sample, cache_pool, cache_k, cache_v, cache_logf = map(A, (x_prompt, x_sample, cache_pool, cache_k, cache_v, cache_logf))
    w_ffn_in, w_ffn_out, w_pool, w_kv, w_fgate, w_q, w_o = map(A, (w_ffn_in, w_ffn_out, w_pool, w_kv, w_fgate, w_q, w_o))
    stage = _NC_CACHE.get("stage", 99)
    if ("nc", stage) not in _NC_CACHE:
        _NC_CACHE[("nc", stage)] = build_program(stage, small=_NC_CACHE.get("small", False), sub=_NC_CACHE.get("sub", 0))
    nc = _NC_CACHE[("nc", stage)]
    wi = w_ffn_in.reshape(4, 8, 128, 2, 22, 128)
    win = np.ascontiguousarray(wi.transpose(0, 4, 2, 3, 1, 5)).reshape(4, 22, 128, 2048)
    wo_ = w_ffn_out.reshape(4, 2, 11, 128, 8, 128)
    wout = np.ascontiguousarray(wo_.transpose(0, 1, 4, 3, 2, 5)).reshape(4, 2, 8, 128, 1408)
    wp = w_pool[0].reshape(4, 2, 128, 256)
    wpool = np.ascontiguousarray(wp.transpose(2, 0, 1, 3)).reshape(128, 2048)
    wkv = np.ascontiguousarray(w_kv.reshape(8, 128, 2048).transpose(1, 0, 2)).reshape(128, 8 * 2048)
    wfg = np.ascontiguousarray(w_fgate.reshape(8, 128, 16).transpose(1, 0, 2)).reshape(128, 128)
    wq = np.ascontiguousarray(w_q[0].reshape(8, 128, 16, 64).transpose(2, 1, 0, 3)).reshape(16, 128, 512)
    wo = np.ascontiguousarray(w_o[0].reshape(16, 64, 1024))
    gains = [ln_ffn1[0], ln_mix[0], ln_ffn2[0], ln_kv, ln_ffn1[1], ln_mix[1], ln_ffn2[1], ln_final, pool_scale[0]]
    gall = np.ascontiguousarray(np.stack([A(g).reshape(8, 128).T for g in gains], axis=1)).reshape(128, 72)
    bfg = np.ascontiguousarray(np.broadcast_to(A(b_fgate)[None, :], (128, 16)))
    ident = np.eye(128, dtype=f32)
    tri = np.triu(np.ones((128, 128), f32))
    pens = np.zeros((128, 64), f32)
    pp_, jj_ = np.meshgrid(np.arange(128), np.arange(64), indexing="ij")
    pens[pp_ > jj_] = -30000.0
    in_maps = []
    for c in range(8):
        b, r = c // 4, c % 4
        xc = np.zeros((TN, 1024), f32)
        for m in range(4):
            g0 = 512 * (4 * m + r)
            if g0 >= 16:
                xc[16 * m:16 * m + 16] = x_prompt[b, g0 - 16:g0]
            xc[64 + 512 * m:64 + 512 * (m + 1)] = x_prompt[b, g0:g0 + 512]
        xc[S0:TN] = x_sample[c]
        cpool = np.zeros((16, 1024), f32)
        cpool[1:16] = cache_pool[0, c]
        clf = np.ascontiguousarray(cache_logf[c].reshape(16, 128, 16).transpose(1, 0, 2)).reshape(128, 256)
        invc = np.zeros((128, 4, 16), f32)
        for g in range(4):
            w = 2 ** (g + 1)
            pos = 512 * r + np.arange(16)
            invc[:, g, :] = (1.0 / np.minimum(pos + 1, w)).astype(f32)[None, :]
        mrank = np.zeros((128, 4), f32)
        mrank[:, r] = 1.0
        pen = np.zeros((16, 128, 512), f32)
        kk = (128 * np.arange(16)[:, None, None] + np.arange(128)[None, :, None])
        qq = 512 * r + np.arange(512)[None, None, :]
        pen[np.broadcast_to(kk > qq, pen.shape)] = -30000.0
        in_maps.append(dict(
            xc=xc, cpool=cpool, ck_in=cache_k[c].reshape(2048, 1024), cv_in=cache_v[c].reshape(2048, 1024), clf=clf,
            win=win, wout=wout, wpool=wpool, wkv=wkv, wfg=wfg, wq=wq, wo=wo, gall=gall, bfg=bfg, ident=ident, tri=tri,
            invc=invc.reshape(128, 64), mrank=mrank, pen=pen, pens=pens))
    if _NC_CACHE.get("maps_only"):
        return in_maps
    if _NC_CACHE.get("small", False):
        for mp in in_maps:
            mp["win"] = np.ascontiguousarray(win[0:1, 0:1])
            mp["wout"] = np.ascontiguousarray(wout[0:1, 0:1, 0:1])
    res = run_bass_kernel_spmd(nc, in_maps, core_ids=list(range(8)))
    R = res.results
    y_prompt = np.zeros((2, 8192, 1024), f32)
    k_prompt = np.zeros((2, 8192, 1024), f32)
    v_prompt = np.zeros((2, 8192, 1024), f32)
    logf_prompt = np.zeros((2, 8192, 16), f32)
    for c in range(8):
        b, r = c // 4, c % 4
        for m in range(4):
            g0 = 512 * (4 * m + r)
            sl = slice(512 * m, 512 * (m + 1))
            y_prompt[b, g0:g0 + 512] = R[c]["y_p"][sl]
            k_prompt[b, g0:g0 + 512] = R[c]["k_p"][sl]
            v_prompt[b, g0:g0 + 512] = R[c]["v_p"][sl]
            logf_prompt[b, g0:g0 + 512] = R[c]["lf_p"][sl]
    y_sample = np.stack([R[c]["y_s"] for c in range(8)])
    pool_prompt = np.stack([R[3]["pool_p"][1:16], R[7]["pool_p"][1:16]])[None]
    pool_sample = np.stack([R[c]["pool_s"][1:16] for c in range(8)])[None]
    k_sample = np.stack([R[c]["k_s"] for c in range(8)]).reshape(8, 64, 16, 64)
    v_sample = np.stack([R[c]["v_s"] for c in range(8)]).reshape(8, 64, 16, 64)
    logf_sample = np.stack([R[c]["lf_s"] for c in range(8)])
    return (y_prompt, y_sample, pool_prompt.astype(f32), pool_sample.astype(f32),
            k_prompt.reshape(2, 8192, 16, 64), v_prompt.reshape(2, 8192, 16, 64), logf_prompt,
            k_sample, v_sample, logf_sample)
```
